# Optimizing a Trainium2 kernel written in Bass

```python
import jax, jax.numpy as jnp
from jax import lax
import numpy as np

D_MODEL = 1024
BATCH = 32
SEQ = 2048
DEPTH = 1

HEAD_DIM = 64
ROPE_THETA = 10000.0
BAND_BLOCK = 128
A_Q_HEADS = 8
A_KV_HEADS = 2
A_REP = A_Q_HEADS // A_KV_HEADS
A_WINDOW = 128
B_Q_HEADS = 8
B_KV_HEADS = 2
B_REP = B_Q_HEADS // B_KV_HEADS
CMP_BLOCK = 32
CMP_STRIDE = 16
CMP_HIDDEN = 256
SLC_BLOCK = 64
SLC_TOP_N = 16
B_WINDOW = 512
FFN_HIDDEN = ((-(-8 * D_MODEL // 3) + 255) // 256) * 256
ALPHA = (2 * DEPTH) ** 0.25
BETA = (8 * DEPTH) ** -0.25
LN_EPS = 1e-5
NEG = -1e30
BIG = 1e9

A_Q = A_Q_HEADS * HEAD_DIM
A_KV = A_KV_HEADS * HEAD_DIM
B_Q = B_Q_HEADS * HEAD_DIM
B_KV = B_KV_HEADS * HEAD_DIM
N_NSA_GATES = 3 * B_Q_HEADS
SPLIT_SIZES = (A_Q, A_KV, A_KV, B_Q, B_KV, B_KV, B_KV, B_KV, B_KV, B_KV, N_NSA_GATES, 2 * D_MODEL)
D_IN = A_Q + 2 * A_KV + B_Q + 6 * B_KV + N_NSA_GATES + 2 * D_MODEL

kernel_name = "hybrid_swa_sink_nsa_gated_deepnorm"


def layer_norm(x, g, b):
    xf = x.astype(jnp.float32)
    mu = jnp.mean(xf, axis=-1, keepdims=True)
    var = jnp.mean(jnp.square(xf - mu), axis=-1, keepdims=True)
    y = (xf - mu) * lax.rsqrt(var + LN_EPS)
    return (y * g + b).astype(x.dtype)


def rope(x, pos):
    half = x.shape[-1] // 2
    inv = ROPE_THETA ** (-jnp.arange(half, dtype=jnp.float32) / half)
    ang = pos.astype(jnp.float32)[:, None] * inv[None, :]
    cos, sin = jnp.cos(ang), jnp.sin(ang)
    xf = x.astype(jnp.float32)
    x1, x2 = xf[..., :half], xf[..., half:]
    return jnp.concatenate([x1 * cos - x2 * sin, x2 * cos + x1 * sin], axis=-1).astype(x.dtype)


def heads_q(t, n_kv, rep):
    b, s, _ = t.shape
    return t.reshape(b, s, n_kv, rep, HEAD_DIM).transpose(0, 2, 3, 1, 4)


def heads_kv(t, n_kv):
    b, s, _ = t.shape
    return t.reshape(b, s, n_kv, HEAD_DIM).transpose(0, 2, 1, 3)


def merge_heads(o):
    b, g, r, s, d = o.shape
    return o.transpose(0, 3, 1, 2, 4).reshape(b, s, g * r * d)


def banded_attention(q, k, v, window, sinks):
    b, g, r, s, d = q.shape
    nq = s // BAND_BLOCK
    nb = window // BAND_BLOCK
    pad = ((0, 0), (0, 0), (window, 0), (0, 0))
    kp = jnp.pad(k, pad).reshape(b, g, nq + nb, BAND_BLOCK, d)
    vp = jnp.pad(v, pad).reshape(b, g, nq + nb, BAND_BLOCK, d)
    kband = jnp.concatenate([kp[:, :, j:j + nq] for j in range(nb + 1)], axis=3)
    vband = jnp.concatenate([vp[:, :, j:j + nq] for j in range(nb + 1)], axis=3)
    qb = q.reshape(b, g, r, nq, BAND_BLOCK, d)
    sc = jnp.einsum("bgrnqd,bgnkd->bgrnqk", qb, kband).astype(jnp.float32) * (d ** -0.5)
    blk = jnp.arange(nq)[:, None, None] * BAND_BLOCK
    qpos = blk + jnp.arange(BAND_BLOCK)[None, :, None]
    kpos = blk - window + jnp.arange((nb + 1) * BAND_BLOCK)[None, None, :]
    mask = (kpos <= qpos) & (qpos - kpos < window) & (kpos >= 0)
    sc = jnp.where(mask, sc, NEG)
    if sinks is None:
        p = jax.nn.softmax(sc, axis=-1)
    else:
        sk = sinks.astype(jnp.float32)[None, :, :, None, None, None]
        m = jnp.maximum(jnp.max(sc, axis=-1, keepdims=True), sk)
        e = jnp.exp(sc - m)
        p = e / (jnp.sum(e, axis=-1, keepdims=True) + jnp.exp(sk - m))
    o = jnp.einsum("bgrnqk,bgnkd->bgrnqd", p.astype(v.dtype), vband)
    return o.reshape(b, g, r, s, d)


def compress(k, pe, w1, b1, w2):
    b, g, s, d = k.shape
    ratio = CMP_BLOCK // CMP_STRIDE
    nc = s // CMP_STRIDE - ratio + 1
    chunks = k.reshape(b, g, s // CMP_STRIDE, CMP_STRIDE, d)
    blocks = jnp.concatenate([chunks[:, :, j:j + nc] for j in range(ratio)], axis=3)
    flat = (blocks + pe).reshape(b, g, nc, CMP_BLOCK * d)
    return jax.nn.gelu(flat @ w1 + b1) @ w2


def compressed_attention(q, kc, vc):
    s = q.shape[3]
    nc = kc.shape[2]
    sc = jnp.einsum("bgrsd,bgcd->bgrsc", q, kc).astype(jnp.float32) * (HEAD_DIM ** -0.5)
    c_end = jnp.arange(nc) * CMP_STRIDE + CMP_BLOCK - 1
    valid = c_end[None, :] <= jnp.arange(s)[:, None]
    sc = jnp.where(valid, sc, NEG)
    e = jnp.exp(sc - jnp.max(sc, axis=-1, keepdims=True)) * valid
    p = e / jnp.maximum(jnp.sum(e, axis=-1, keepdims=True), 1e-30)
    o = jnp.einsum("bgrsc,bgcd->bgrsd", p.astype(vc.dtype), vc)
    return o, p


def select_blocks(p_cmp):
    s, nc = p_cmp.shape[3], p_cmp.shape[4]
    nb = s // SLC_BLOCK
    c_start = jnp.arange(nc) * CMP_STRIDE
    s_start = jnp.arange(nb) * SLC_BLOCK
    overlap = ((c_start[:, None] < s_start[None, :] + SLC_BLOCK)
               & (s_start[None, :] < c_start[:, None] + CMP_BLOCK)).astype(jnp.float32)
    imp = jnp.einsum("bgrsc,cj->bgsj", p_cmp, overlap)
    cur = (jnp.arange(s) // SLC_BLOCK)[:, None]
    j = jnp.arange(nb)[None, :]
    forced = (j == 0) | (j == cur) | (j == cur - 1)
    score = jnp.where(forced, BIG, jnp.where(j <= cur, imp, -BIG))
    _, idx = lax.top_k(score, min(SLC_TOP_N, nb))
    return idx


def selected_attention(q, k, v, idx):
    b, g, r, s, d = q.shape
    n = idx.shape[-1]
    nb = s // SLC_BLOCK
    nq = s // SLC_BLOCK
    kb = k.reshape(b, g, nb, SLC_BLOCK, d)
    vb = v.reshape(b, g, nb, SLC_BLOCK, d)
    q_blk = q.reshape(b, g, r, nq, SLC_BLOCK, d).transpose(3, 0, 1, 2, 4, 5)
    i_blk = idx.reshape(b, g, nq, SLC_BLOCK, n).transpose(2, 0, 1, 3, 4)
    starts = jnp.arange(nq) * SLC_BLOCK
    gather = jax.vmap(jax.vmap(lambda blocks, ids: blocks[ids]))

    def one_block(args):
        qb, ib, st = args
        kg = gather(kb, ib).reshape(b, g, SLC_BLOCK, n * SLC_BLOCK, d)
        vg = gather(vb, ib).reshape(b, g, SLC_BLOCK, n * SLC_BLOCK, d)
        kpos = (ib[..., None] * SLC_BLOCK + jnp.arange(SLC_BLOCK)).reshape(b, g, SLC_BLOCK, n * SLC_BLOCK)
        qpos = st + jnp.arange(SLC_BLOCK)
        mask = (kpos <= qpos[None, None, :, None])[:, :, None]
        sc = jnp.einsum("bgrqd,bgqkd->bgrqk", qb, kg).astype(jnp.float32) * (d ** -0.5)
        p = jax.nn.softmax(jnp.where(mask, sc, NEG), axis=-1)
        return jnp.einsum("bgrqk,bgqkd->bgrqd", p.astype(vg.dtype), vg)

    out = lax.map(one_block, (q_blk, i_blk, starts))
    return out.transpose(1, 2, 3, 0, 4, 5).reshape(b, g, r, s, d)


def token_mixer(u, w_in, sinks, cmp_pe_k, cmp_w1_k, cmp_b1_k, cmp_w2_k,
                cmp_pe_v, cmp_w1_v, cmp_b1_v, cmp_w2_v, w_proj_a, w_proj_b, w_out):
    b, s, _ = u.shape
    pos = jnp.arange(s)
    z = u @ w_in
    cuts = [int(c) for c in np.cumsum(SPLIT_SIZES)[:-1]]
    qa, ka, va, qn, kc, vc, ksl, vsl, kw, vw, g_nsa, g_merge = jnp.split(z, cuts, axis=-1)

    qa = rope(heads_q(qa, A_KV_HEADS, A_REP), pos)
    ka = rope(heads_kv(ka, A_KV_HEADS), pos)
    oa = banded_attention(qa, ka, heads_kv(va, A_KV_HEADS), A_WINDOW, sinks.reshape(A_KV_HEADS, A_REP))
    oa = merge_heads(oa)

    qn = heads_q(qn, B_KV_HEADS, B_REP)
    qn_rot = rope(qn, pos)
    kcmp = compress(heads_kv(kc, B_KV_HEADS), cmp_pe_k, cmp_w1_k, cmp_b1_k, cmp_w2_k)
    vcmp = compress(heads_kv(vc, B_KV_HEADS), cmp_pe_v, cmp_w1_v, cmp_b1_v, cmp_w2_v)
    o_cmp, p_cmp = compressed_attention(qn, kcmp, vcmp)
    idx = select_blocks(p_cmp)
    o_slc = selected_attention(qn_rot, rope(heads_kv(ksl, B_KV_HEADS), pos), heads_kv(vsl, B_KV_HEADS), idx)
    o_win = banded_attention(qn_rot, rope(heads_kv(kw, B_KV_HEADS), pos), heads_kv(vw, B_KV_HEADS), B_WINDOW, None)
    gn = jax.nn.sigmoid(g_nsa).reshape(b, s, 3, B_KV_HEADS, B_REP).transpose(2, 0, 3, 4, 1)[..., None]
    ob = merge_heads(gn[0] * o_cmp + gn[1] * o_slc + gn[2] * o_win)

    gm = jax.nn.sigmoid(g_merge).reshape(b, s, 2, D_MODEL)
    y = gm[:, :, 0] * (oa @ w_proj_a) + gm[:, :, 1] * (ob @ w_proj_b)
    return y @ w_out


def setup_inputs(seed: int = 0) -> dict:
    key = jax.random.key(seed)
    ks = jax.random.split(key, 24)
    L = DEPTH

    def nrm(k, shape, scale):
        return jax.random.normal(k, shape, jnp.float32) * scale

    fan_c = CMP_BLOCK * HEAD_DIM
    return {
        "x": nrm(ks[0], (BATCH, SEQ, D_MODEL), 1.0),
        "w_in": nrm(ks[1], (L, D_MODEL, D_IN), D_MODEL ** -0.5),
        "sinks": nrm(ks[2], (L, A_Q_HEADS), 0.5),
        "cmp_pe_k": nrm(ks[3], (L, CMP_BLOCK, HEAD_DIM), 0.1),
        "cmp_w1_k": nrm(ks[4], (L, fan_c, CMP_HIDDEN), fan_c ** -0.5),
        "cmp_b1_k": nrm(ks[5], (L, CMP_HIDDEN), 0.01),
        "cmp_w2_k": nrm(ks[6], (L, CMP_HIDDEN, HEAD_DIM), CMP_HIDDEN ** -0.5),
        "cmp_pe_v": nrm(ks[7], (L, CMP_BLOCK, HEAD_DIM), 0.1),
        "cmp_w1_v": nrm(ks[8], (L, fan_c, CMP_HIDDEN), fan_c ** -0.5),
        "cmp_b1_v": nrm(ks[9], (L, CMP_HIDDEN), 0.01),
        "cmp_w2_v": nrm(ks[10], (L, CMP_HIDDEN, HEAD_DIM), CMP_HIDDEN ** -0.5),
        "w_proj_a": nrm(ks[11], (L, A_Q, D_MODEL), A_Q ** -0.5),
        "w_proj_b": nrm(ks[12], (L, B_Q, D_MODEL), B_Q ** -0.5),
        "w_out": nrm(ks[13], (L, D_MODEL, D_MODEL), BETA * D_MODEL ** -0.5),
        "ln1_g": 1.0 + nrm(ks[14], (L, D_MODEL), 0.01),
        "ln1_b": nrm(ks[15], (L, D_MODEL), 0.01),
        "w_gate": nrm(ks[16], (L, D_MODEL, FFN_HIDDEN), D_MODEL ** -0.5),
        "w_up": nrm(ks[17], (L, D_MODEL, FFN_HIDDEN), D_MODEL ** -0.5),
        "w_down": nrm(ks[18], (L, FFN_HIDDEN, D_MODEL), BETA * FFN_HIDDEN ** -0.5),
        "ln2_g": 1.0 + nrm(ks[19], (L, D_MODEL), 0.01),
        "ln2_b": nrm(ks[20], (L, D_MODEL), 0.01),
    }


def reference(x, w_in, sinks, cmp_pe_k, cmp_w1_k, cmp_b1_k, cmp_w2_k, cmp_pe_v, cmp_w1_v, cmp_b1_v,
              cmp_w2_v, w_proj_a, w_proj_b, w_out, ln1_g, ln1_b, w_gate, w_up, w_down, ln2_g, ln2_b):
    for l in range(DEPTH):
        m = token_mixer(x, w_in[l], sinks[l], cmp_pe_k[l], cmp_w1_k[l], cmp_b1_k[l], cmp_w2_k[l],
                        cmp_pe_v[l], cmp_w1_v[l], cmp_b1_v[l], cmp_w2_v[l], w_proj_a[l], w_proj_b[l], w_out[l])
        h = layer_norm(ALPHA * x + m, ln1_g[l], ln1_b[l])
        f = (jax.nn.silu(h @ w_gate[l]) * (h @ w_up[l])) @ w_down[l]
        x = layer_norm(ALPHA * h + f, ln2_g[l], ln2_b[l])
    return x
```

```python
import numpy as np
import concourse.bass as bass
import concourse.mybir as mybir
from concourse.bass_utils import run_bass_kernel_spmd

F32 = mybir.dt.float32
BF16 = mybir.dt.bfloat16
AF = mybir.ActivationFunctionType
ALU = mybir.AluOpType

D = 1024
S = 2048
NT = 16
DH = 64
FFN = 2816
NHC = 22
ALPHA = 2.0 ** 0.25
LN_EPS = 1e-5
NEG = -30000.0
BIG = 1e9
NCORES = 8
SEQ_PER_CORE = 4
GELU_C = 0.7978845608028654


class Buf:
    __slots__ = ("name", "w", "r", "excl")

    def __init__(self, name, excl=False):
        self.name = name
        self.w = None
        self.r = {}
        self.excl = excl


class Sched:
    ENG = ("pe", "act", "dve", "pool", "sp")

    def __init__(self, nc, sems, dma_sems):
        self.nc = nc
        self.eng = {"pe": nc.tensor, "act": nc.scalar, "dve": nc.vector, "pool": nc.gpsimd, "sp": nc.sync}
        self.sem = dict(sems)
        self.cnt = {e: 0 for e in self.ENG}
        self.seen = {e: {} for e in self.ENG}
        self.dma_keys = {q: [] for q in ("sp", "pool")}
        self.dma_val = {}
        self.dma_next = {"sp": 0, "pool": 0}
        for q, lst in dma_sems.items():
            for i, h in enumerate(lst):
                k = "d%s%d" % (q, i)
                self.sem[k] = h
                self.dma_keys[q].append(k)
                self.dma_val[k] = 0
        self.n_inst = 0

    def _wait(self, e, key, val):
        if val <= 0:
            return
        if self.seen[e].get(key, 0) >= val:
            return
        self.eng[e].wait_ge(self.sem[key], val)
        self.seen[e][key] = val

    def _deps(self, e, reads, writes):
        need = {}
        for b in reads:
            if b.w is not None:
                k, v = b.w
                if need.get(k, 0) < v:
                    need[k] = v
            if b.excl:
                for k, v in b.r.items():
                    if k != e and need.get(k, 0) < v:
                        need[k] = v
        for b in writes:
            if b.w is not None:
                k, v = b.w
                if (k != e or e != "pe") and need.get(k, 0) < v:
                    need[k] = v
            for k, v in b.r.items():
                if need.get(k, 0) < v:
                    need[k] = v
        for k, v in need.items():
            self._wait(e, k, v)

    def op(self, e, fn, reads=(), writes=()):
        self._deps(e, reads, writes)
        inst = fn()
        self.cnt[e] += 1
        c = self.cnt[e]
        inst.then_inc(self.sem[e], 1)
        self.seen[e][e] = max(self.seen[e].get(e, 0), 0)
        for b in reads:
            if b.r.get(e, 0) < c:
                b.r[e] = c
        for b in writes:
            b.w = (e, c)
            b.r = {}
        self.n_inst += 1
        return inst

    def dma(self, q, out, in_, reads=(), writes=()):
        key = self.dma_keys[q][self.dma_next[q] % len(self.dma_keys[q])]
        self.dma_next[q] += 1
        self._wait(q, key, self.dma_val[key])
        self._deps(q, reads, writes)
        self.dma_val[key] += 16
        v = self.dma_val[key]
        self.eng[q].dma_start(out=out, in_=in_).then_inc(self.sem[key], 16)
        for b in reads:
            b.r[key] = v
        for b in writes:
            b.w = (key, v)
            b.r = {}
        self.n_inst += 1

    def barrier(self, engines=None):
        engines = engines or self.ENG
        for e in self.ENG:
            for o in self.ENG:
                if o != e:
                    self._wait(e, o, self.cnt[o])
            for k, v in self.dma_val.items():
                self._wait(e, k, v)

    def final_wait(self, e="sp"):
        for k, v in self.dma_val.items():
            self._wait(e, k, v)
        for o in self.ENG:
            if o != e:
                self._wait(e, o, self.cnt[o])


COL = dict(qa=0, ka=512, va=640, qn=768, kc=1280, vc=1408, ksl=1536, vsl=1664, kw=1792, vw=1920,
           gn=2048, gm=2072)


def _blk(w):
    m = w.shape[1]
    return np.ascontiguousarray(w.reshape(8, 128, m).transpose(1, 0, 2))


def _swap(u):
    return np.concatenate([u[:, 32:64], u[:, 0:32]], axis=1)


def host_consts():
    c = {}
    c["ident"] = np.eye(128, dtype=np.float32)
    k = np.arange(128)[:, None]
    q = np.arange(128)[None, :]
    d1 = np.where(k <= q, 0.0, NEG).astype(np.float32)
    u1 = np.where(k > q, 0.0, NEG).astype(np.float32)
    c["dmask"] = np.ascontiguousarray(np.tile(d1, (1, 4)))
    c["umask"] = np.ascontiguousarray(np.tile(u1, (1, 4)))
    cc = np.arange(128)[:, None]
    t = np.arange(S)[None, :]
    c["cmpbias"] = np.where(16 * cc + 31 <= t, 0.0, NEG).astype(np.float32)
    half = DH // 2
    inv = 10000.0 ** (-np.arange(half, dtype=np.float64) / half)
    ang = np.arange(S, dtype=np.float64)[:, None] * inv[None, :]
    cos = np.cos(ang).astype(np.float32).T
    sin = np.sin(ang).astype(np.float32).T
    c["ropec"] = np.ascontiguousarray(np.concatenate([cos, cos, cos, cos], axis=0))
    c["ropes"] = np.ascontiguousarray(np.concatenate([-sin, sin, -sin, sin], axis=0))
    tt = (np.arange(NT)[None, :, None] * 128 + np.arange(128)[:, None, None])
    cur = tt // 64
    j = np.arange(32)[None, None, :]
    forced = (j == 0) | (j == cur) | (j == cur - 1)
    allowed = (~forced) & (j <= cur)
    c["sel_allowed"] = np.ascontiguousarray(np.broadcast_to(allowed, (128, NT, 32)).astype(np.float32))
    fb = np.where(forced, BIG, np.where(j <= cur, 0.0, -BIG)).astype(np.float32)
    c["sel_bias"] = np.ascontiguousarray(np.broadcast_to(fb, (128, NT, 32)).astype(np.float32))
    ncmp = 127
    c_start = np.arange(ncmp) * 16
    s_start = np.arange(32) * 64
    ov = ((c_start[:, None] < s_start[None, :] + 64) & (s_start[None, :] < c_start[:, None] + 32)).astype(np.float32)
    ova = np.zeros((128, 33), np.float32)
    ova[:ncmp, 0] = 1.0
    ova[:ncmp, 1:] = ov
    c["ovaug"] = ova
    c["eblk"] = (np.arange(S)[None, :] // 64 == np.arange(32)[:, None]).astype(np.float32)
    return c


def host_weights(inp):
    w_in = np.asarray(inp["w_in"][0], np.float32)
    out = {}

    def unit(name, idx):
        c0 = COL[name] + idx * 64
        return w_in[:, c0:c0 + 64]

    wfa = np.zeros((2, 6, 128, 8, 128), np.float32)
    wfb = np.zeros((2, 7, 128, 8, 128), np.float32)
    wta = np.zeros((2, 128, 8, 64), np.float32)
    wtb = np.zeros((2, 128, 8, 140), np.float32)
    for g in range(2):
        a1 = np.concatenate([unit("qa", 4 * g + 0), unit("qa", 4 * g + 1)], 1)
        a1s = np.concatenate([_swap(unit("qa", 4 * g + 0)), _swap(unit("qa", 4 * g + 1))], 1)
        a2 = np.concatenate([unit("qa", 4 * g + 2), unit("qa", 4 * g + 3)], 1)
        a2s = np.concatenate([_swap(unit("qa", 4 * g + 2)), _swap(unit("qa", 4 * g + 3))], 1)
        a3 = np.concatenate([unit("ka", g), unit("ka", g)], 1)
        a3s = np.concatenate([_swap(unit("ka", g)), _swap(unit("ka", g))], 1)
        for i, b in enumerate([a1, a1s, a2, a2s, a3, a3s]):
            wfa[g, i] = _blk(b)
        wta[g] = _blk(unit("va", g))
        b1 = np.concatenate([unit("qn", 4 * g + 0), unit("qn", 4 * g + 1)], 1)
        b1s = np.concatenate([_swap(unit("qn", 4 * g + 0)), _swap(unit("qn", 4 * g + 1))], 1)
        b2 = np.concatenate([unit("qn", 4 * g + 2), unit("qn", 4 * g + 3)], 1)
        b2s = np.concatenate([_swap(unit("qn", 4 * g + 2)), _swap(unit("qn", 4 * g + 3))], 1)
        b3 = np.concatenate([unit("ksl", g), unit("kw", g)], 1)
        b3s = np.concatenate([_swap(unit("ksl", g)), _swap(unit("kw", g))], 1)
        b4 = np.concatenate([unit("kc", g), unit("vc", g)], 1)
        for i, b in enumerate([b1, b1s, b2, b2s, b3, b3s, b4]):
            wfb[g, i] = _blk(b)
        gcols = [COL["gn"] + br * 8 + 4 * g + r for br in range(3) for r in range(4)]
        tb = np.concatenate([unit("vsl", g), unit("vw", g), w_in[:, gcols]], 1)
        wtb[g] = _blk(tb)
    out["wfa"] = wfa
    out["wfb"] = wfb
    out["wta"] = wta
    out["wtb"] = wtb
    gm = w_in[:, COL["gm"]:COL["gm"] + 2048]
    out["wgm"] = np.ascontiguousarray(np.stack([_blk(gm[:, i * 128:(i + 1) * 128]) for i in range(16)]))
    out["wpa"] = np.ascontiguousarray(np.asarray(inp["w_proj_a"][0], np.float32).reshape(4, 128, D).transpose(1, 0, 2))
    out["wpb"] = np.ascontiguousarray(np.asarray(inp["w_proj_b"][0], np.float32).reshape(4, 128, D).transpose(1, 0, 2))
    out["wout"] = _blk(np.asarray(inp["w_out"][0], np.float32))
    wg = np.asarray(inp["w_gate"][0], np.float32)
    wu = np.asarray(inp["w_up"][0], np.float32)
    out["wg"] = np.ascontiguousarray(np.stack([_blk(wg[:, i * 128:(i + 1) * 128]) for i in range(NHC)]))
    out["wu"] = np.ascontiguousarray(np.stack([_blk(wu[:, i * 128:(i + 1) * 128]) for i in range(NHC)]))
    out["wd"] = np.ascontiguousarray(np.asarray(inp["w_down"][0], np.float32).reshape(NHC, 128, D).transpose(1, 0, 2))
    for kv in ("k", "v"):
        w1 = np.asarray(inp["cmp_w1_" + kv][0], np.float32)
        out["w1" + kv] = np.ascontiguousarray(w1.reshape(32, 64, 256).transpose(1, 0, 2))
        w2 = np.asarray(inp["cmp_w2_" + kv][0], np.float32)
        out["w2" + kv] = np.ascontiguousarray(w2.reshape(2, 128, 64).transpose(1, 0, 2))
        b1 = np.asarray(inp["cmp_b1_" + kv][0], np.float32)
        out["b1" + kv] = np.ascontiguousarray(b1.reshape(2, 128).T)
        pe = np.asarray(inp["cmp_pe_" + kv][0], np.float32)
        out["pet" + kv] = np.ascontiguousarray(pe.T)
    out["sinks"] = np.ascontiguousarray(np.broadcast_to(np.asarray(inp["sinks"][0], np.float32)[None, :], (128, 8)))
    for nm in ("ln1_g", "ln1_b", "ln2_g", "ln2_b"):
        out[nm] = np.ascontiguousarray(np.broadcast_to(np.asarray(inp[nm][0], np.float32)[None, :], (128, D)))
    return out


A_XT = 0
A_OT = 32768
A_H = 32768
A_QK = 65536
A_VV = 114688
A_YT = 98304
A_ACT = 98304
A_W = 131072
A_D = 143360
ARENA_BYTES = 194560

CONST_SPECS = [
    ("ident", [128, 128]), ("dmask", [128, 512]), ("umask", [128, 512]), ("cmpbias", [128, S]),
    ("ropec", [128, S]), ("ropes", [128, S]), ("sel_allowed", [128, NT, 32]), ("sel_bias", [128, NT, 32]),
    ("ovaug", [128, 33]), ("eblk", [32, S]),
]
WEIGHT_SPECS = [
    ("wfa", [2, 6, 128, 8, 128]), ("wfb", [2, 7, 128, 8, 128]), ("wta", [2, 128, 8, 64]), ("wtb", [2, 128, 8, 140]),
    ("wgm", [16, 128, 8, 128]), ("wpa", [128, 4, D]), ("wpb", [128, 4, D]), ("wout", [128, 8, D]),
    ("wg", [NHC, 128, 8, 128]), ("wu", [NHC, 128, 8, 128]), ("wd", [128, NHC, D]),
    ("w1k", [64, 32, 256]), ("w1v", [64, 32, 256]), ("w2k", [128, 2, 64]), ("w2v", [128, 2, 64]),
    ("b1k", [128, 2]), ("b1v", [128, 2]), ("petk", [64, 32]), ("petv", [64, 32]), ("sinks", [128, 8]),
    ("ln1_g", [128, D]), ("ln1_b", [128, D]), ("ln2_g", [128, D]), ("ln2_b", [128, D]),
]


class KB:
    def __init__(self, nseq, stop_after=None, dbg=()):
        self.nseq = nseq
        self.stop_after = stop_after
        self.dbg = set(dbg)
        self.nc = bass.Bass("TRN2", target_bir_lowering=False)
        nc = self.nc
        self.dr = {}
        for nm, shp in CONST_SPECS + WEIGHT_SPECS:
            self.dr[nm] = nc.dram_tensor(nm, list(shp), F32, kind="ExternalInput").ap()
        self.dr["x"] = nc.dram_tensor("x", [nseq, S, D], F32, kind="ExternalInput").ap()
        self.dr["xt"] = nc.dram_tensor("xt", [nseq, D, S], F32, kind="ExternalInput").ap()
        self.dr["out"] = nc.dram_tensor("out", [nseq, S, D], F32, kind="ExternalOutput").ap()
        self.dbg_out = {}

    def view(self, off, shape, dt, parts=128, p0=0):
        esz = 4 if dt == F32 else 2
        n = int(np.prod(shape[1:]))
        assert off % 4 == 0
        ap = self.arena[p0:p0 + parts, off // 2: off // 2 + n * esz // 2]
        if dt == F32:
            ap = ap.bitcast(F32)
        if len(shape) == 3:
            ap = ap.rearrange("p (a b) -> p a b", b=shape[2])
        elif len(shape) == 4:
            ap = ap.rearrange("p (a b c) -> p a b c", b=shape[2], c=shape[3])
        return ap

    def mm(self, out, lhsT, rhs, start, stop, reads, writes, **kw):
        nc = self.nc
        return self.s.op("pe", lambda: nc.tensor.matmul(out, lhsT, rhs, start=start, stop=stop, **kw), reads, writes)

    def tr(self, out, in_, reads, writes, **kw):
        nc = self.nc
        return self.s.op("pe", lambda: nc.tensor.transpose(out, in_, self.ident[:], **kw), reads + [self.b_const], writes)

    def act(self, out, in_, func, reads, writes, **kw):
        nc = self.nc
        return self.s.op("act", lambda: nc.scalar.activation(out=out, in_=in_, func=func, **kw), reads, writes)

    def tt(self, out, in0, in1, op, reads, writes, eng="dve"):
        e = self.s.eng[eng]
        return self.s.op(eng, lambda: e.tensor_tensor(out=out, in0=in0, in1=in1, op=op), reads, writes)

    def ts(self, out, in0, s1, s2, op0, op1, reads, writes, eng="dve"):
        e = self.s.eng[eng]
        if op1 is None:
            return self.s.op(eng, lambda: e.tensor_scalar(out=out, in0=in0, scalar1=s1, scalar2=None, op0=op0), reads, writes)
        return self.s.op(eng, lambda: e.tensor_scalar(out=out, in0=in0, scalar1=s1, scalar2=s2, op0=op0, op1=op1), reads, writes)

    def stt(self, out, in0, scalar, in1, op0, op1, reads, writes):
        nc = self.nc
        return self.s.op("dve", lambda: nc.vector.scalar_tensor_tensor(out=out, in0=in0, scalar=scalar, in1=in1, op0=op0, op1=op1), reads, writes)

    def cp(self, out, in_, reads, writes, eng="dve"):
        if eng == "act":
            nc = self.nc
            return self.s.op("act", lambda: nc.scalar.copy(out=out, in_=in_), reads, writes)
        e = self.s.eng[eng]
        return self.s.op(eng, lambda: e.tensor_copy(out=out, in_=in_), reads, writes)

    def memset(self, ap, val, writes, eng="dve"):
        e = self.s.eng[eng]
        return self.s.op(eng, lambda: e.memset(ap, val), [], writes)

    def dump(self, name, ap, shape, dt, reads):
        if name not in self.dbg:
            return
        if name in self.dbg_out:
            return
        d = self.nc.dram_tensor("dbg_" + name, list(shape), dt, kind="ExternalOutput").ap()
        self.dbg_out[name] = d
        self.s.dma("sp", d, ap, reads=reads, writes=[])

    def build(self):
        nc = self.nc
        from contextlib import ExitStack
        with ExitStack() as es:
            E = es.enter_context
            self.arena = E(nc.sbuf_tensor("sb_arena", [128, ARENA_BYTES // 2], BF16))
            self.ident = E(nc.sbuf_tensor("sb_ident", [128, 128], BF16))
            self.dmask = E(nc.sbuf_tensor("sb_dmask", [128, 512], BF16))
            self.umask = E(nc.sbuf_tensor("sb_umask", [128, 512], BF16))
            self.cmpbias = E(nc.sbuf_tensor("sb_cmpbias", [128, S], BF16))
            self.sel_allowed = E(nc.sbuf_tensor("sb_sel_allowed", [128, NT, 32], F32))
            self.sel_bias = E(nc.sbuf_tensor("sb_sel_bias", [128, NT, 32], F32))
            self.cva = [E(nc.sbuf_tensor("sb_cva%d" % i, [128, 97], BF16)) for i in range(2)]
            self.w2k = E(nc.sbuf_tensor("sb_w2k", [128, 2, 64], BF16))
            self.w2v = E(nc.sbuf_tensor("sb_w2v", [128, 2, 64], BF16))
            self.cbias = E(nc.sbuf_tensor("sb_cbias", [128, 4], F32))
            self.b1 = E(nc.sbuf_tensor("sb_b1", [128, 4], F32))
            self.esink = E(nc.sbuf_tensor("sb_esink", [128, 8], F32))
            self.epsc = E(nc.sbuf_tensor("sb_epsc", [128, 1], F32))
            self.pet = E(nc.sbuf_tensor("sb_pet", [128, 32], BF16))
            self.bpair = [E(nc.psum_tensor("bpair%d" % i, [128, 1024], F32)) for i in range(4)]
            self.banks = [self.bpair[i // 2][:, (i % 2) * 512:(i % 2) * 512 + 512] for i in range(8)]
            sems = {e: E(nc.semaphore("s_" + e)) for e in Sched.ENG}
            dsems = {q: [E(nc.semaphore("d_%s%d" % (q, i))) for i in range(12)] for q in ("sp", "pool")}
            self.s = Sched(nc, sems, dsems)
            self.b_bank = [Buf("bank%d" % i, excl=True) for i in range(8)]
            self.b_const = Buf("const")
            self.b_cbias = Buf("cbias")
            self.body()
            self.s.final_wait("sp")
        return nc

    def body(self):
        s = self.s
        dr = self.dr
        bc = self.b_const
        self.load_xT(0)
        s.dma("pool", self.ident[:], dr["ident"], writes=[bc])
        s.dma("pool", self.dmask[:], dr["dmask"], writes=[bc])
        s.dma("pool", self.umask[:], dr["umask"], writes=[bc])
        s.dma("pool", self.cmpbias[:], dr["cmpbias"], writes=[bc])
        s.dma("sp", self.sel_allowed[:], dr["sel_allowed"], writes=[bc])
        s.dma("sp", self.sel_bias[:], dr["sel_bias"], writes=[bc])
        for i in range(2):
            s.dma("pool", self.cva[i][:, 64:97], dr["ovaug"], writes=[bc])
        s.dma("pool", self.w2k[:], dr["w2k"], writes=[bc])
        s.dma("pool", self.w2v[:], dr["w2v"], writes=[bc])
        s.dma("sp", self.b1[:, 0:2], dr["b1k"], writes=[bc])
        s.dma("sp", self.b1[:, 2:4], dr["b1v"], writes=[bc])
        s.dma("sp", self.esink[:], dr["sinks"], writes=[bc])
        self.act(self.esink[:], self.esink[:], AF.Exp, [bc], [bc])
        self.memset(self.epsc[:], LN_EPS, [bc])
        self.cbias_done = False
        for sq in range(self.nseq):
            self.sequence(sq)
            if self.stop_after is not None:
                break

    def compute_cbias(self):
        s = self.s
        dr = self.dr
        bc = self.b_const
        pet = self.pet
        b_pet = Buf("pet")
        s.dma("pool", pet[0:64], dr["petk"], writes=[b_pet])
        s.dma("pool", pet[64:128], dr["petv"], writes=[b_pet])
        for kv in range(2):
            p0 = 64 * kv
            for hc in range(2):
                bk = self.banks[7]
                for l in range(32):
                    self.mm(bk[:, 0:1], self.w1kv[p0:p0 + 64, l, hc * 128:(hc + 1) * 128], pet[p0:p0 + 64, l:l + 1],
                            l == 0, l == 31, [self.b_w1, b_pet], [self.b_bank[7]])
                self.tt(self.cbias[:, 2 * kv + hc:2 * kv + hc + 1], bk[:, 0:1], self.b1[:, 2 * kv + hc:2 * kv + hc + 1],
                        ALU.add, [self.b_bank[7], bc], [self.b_cbias])
        self.cbias_done = True

    def sequence(self, sq):
        s = self.s
        dr = self.dr
        self.load_xT(sq)
        self.setup_seq_bufs(sq)
        self.pre = {}
        self.pref("a", 0)
        self.w1kv = self.view(A_W + 32768, [128, 32, 256], BF16)
        self.b_w1 = Buf("w1kv")
        s.dma("pool", self.w1kv[0:64], dr["w1k"], writes=[self.b_w1])
        s.dma("pool", self.w1kv[64:128], dr["w1v"], writes=[self.b_w1])
        self.OT = self.view(A_OT, [128, 8, S], BF16)
        self.b_OT = Buf("OT")
        for g in range(2):
            self.mixer_a(sq, g)
            self.pref("b", g)
            if self.stop_after == "a%d" % g:
                self.dump("OTa", self.OT, [128, 8, S], BF16, [self.b_OT])
                return
            s.barrier()
            self.mixer_b(sq, g)
            if g == 0:
                self.pref("a", 1)
            else:
                self.pref("c1", 0)
            if self.stop_after is not None and self.stop_after.startswith("b%d" % g):
                self.dump("OTb", self.OT, [128, 8, S], BF16, [self.b_OT])
                return
            s.barrier()
        self.dump("OT", self.OT, [128, 8, S], BF16, [self.b_OT])
        if self.stop_after == "attn":
            return
        self.phase_c1(sq)
        s.barrier()
        if self.stop_after == "c1":
            return
        self.phase_c2(sq)
        s.barrier()
        if self.stop_after == "c2":
            return
        self.phase_d(sq)
        s.barrier()

    def load_xT(self, sq, war=()):
        if getattr(self, "xT_loaded", None) == sq:
            return
        self.xT_loaded = sq
        xT = self.view(A_XT, [128, 8, S], BF16)
        self.xT = xT
        self.b_xTc = [Buf("xT%d" % i) for i in range(4)]
        src = self.dr["xt"][sq].rearrange("(k p) t -> p k t", p=128)
        for tc in range(4):
            self.s.dma("pool", xT[:, :, tc * 512:(tc + 1) * 512], src[:, :, tc * 512:(tc + 1) * 512],
                       writes=[self.b_xTc[tc]] + list(war))

    def setup_seq_bufs(self, sq, war=()):
        if getattr(self, "seq_bufs_for", None) == sq:
            return
        self.seq_bufs_for = sq
        s = self.s
        self.ropec = self.view(A_W, [128, S], F32)
        self.ropes = self.view(A_W + 8192, [128, S], F32)
        self.b_rope = Buf("rope")
        s.dma("sp", self.ropec, self.dr["ropec"], writes=[self.b_rope] + list(war))
        s.dma("sp", self.ropes, self.dr["ropes"], writes=[self.b_rope] + list(war))
        self.wring = [self.view(A_W + 16384 + 2048 * i, [128, 8, 128], BF16) for i in range(4)]
        self.b_wring = [Buf("wring%d" % i) for i in range(4)]
        self.wt = self.view(A_W + 24576, [128, 8, 140], BF16)
        self.b_wt = Buf("wt")
        self.t1 = self.view(A_W + 27136, [128, 512], F32)
        self.t2 = self.view(A_W + 29184, [128, 512], F32)
        self.b_t1 = Buf("t1")
        self.b_t2 = Buf("t2")
        self.wr_i = 0

    def pref(self, kind, g):
        dr = self.dr
        if kind == "a":
            self.s.dma("pool", self.wt[:, :, 0:64], dr["wta"][g], writes=[self.b_wt])
            self.pre[(kind, g)] = [self.load_wblock(dr["wfa"][g, i]) for i in range(4)]
        elif kind == "b":
            self.s.dma("pool", self.wt[:, :, 0:140], dr["wtb"][g], writes=[self.b_wt])
            self.pre[(kind, g)] = [self.load_wblock(dr["wfb"][g, i]) for i in range(4)]
        else:
            Wpa = self.view(A_W, [128, 4, D], BF16)
            Wpb = self.view(A_W + 8192, [128, 4, D], BF16)
            b_wp = Buf("wp")
            self.s.dma("pool", Wpa, dr["wpa"], writes=[b_wp, self.b_rope])
            self.s.dma("pool", Wpb, dr["wpb"], writes=[b_wp, self.b_rope])
            blk = [self.load_wblock(dr["wgm"][i * 8]) for i in range(2)]
            self.pre["c1"] = (Wpa, Wpb, b_wp, blk)

    def load_wblock(self, src):
        i = self.wr_i % 4
        self.wr_i += 1
        self.s.dma("pool", self.wring[i], src, writes=[self.b_wring[i]])
        return self.wring[i], self.b_wring[i]

    def proj_fm(self, wz, bz, ws, bs, M, dests, bank0):
        xT = self.xT
        for tc in range(4):
            self.fm_rot = (getattr(self, "fm_rot", -1) + 1) % 3
            bz_i = (0, 2, 6)[self.fm_rot]
            bkz = self.banks[bz_i]
            bbz = self.b_bank[bz_i]
            for k in range(8):
                self.mm(bkz[0:M, :], wz[:, k, 0:M], xT[:, k, tc * 512:(tc + 1) * 512], k == 0, k == 7,
                        [bz, self.b_xTc[tc]], [bbz])
            if ws is not None:
                bks = self.banks[bz_i + 1]
                bbs = self.b_bank[bz_i + 1]
                for k in range(8):
                    self.mm(bks[0:M, :], ws[:, k, 0:M], xT[:, k, tc * 512:(tc + 1) * 512], k == 0, k == 7,
                            [bs, self.b_xTc[tc]], [bbs])
            need_rope = any(x[0] == "rope" for dl in dests for x in dl)
            if need_rope:
                self.tt(self.t1[0:M, :], bkz[0:M, :], self.ropec[0:M, tc * 512:(tc + 1) * 512], ALU.mult,
                        [bbz, self.b_rope], [self.b_t1])
                self.tt(self.t2[0:M, :], bks[0:M, :], self.ropes[0:M, tc * 512:(tc + 1) * 512], ALU.mult,
                        [bbs, self.b_rope], [self.b_t2])
            for half, dl in enumerate(dests):
                p0 = 64 * half
                for (kind, dst, bd) in dl:
                    d = dst[:, tc * 512:(tc + 1) * 512]
                    if kind == "rope":
                        self.tt(d, self.t1[p0:p0 + 64, :], self.t2[p0:p0 + 64, :], ALU.add,
                                [self.b_t1, self.b_t2], [bd])
                    else:
                        self.cp(d, bkz[p0:p0 + 64, :], [bbz], [bd], eng=("act" if p0 == 0 else "dve"))

    def mixer_a(self, sq, g):
        s = self.s
        dr = self.dr
        QaT = self.view(A_QK, [128, 4, S], BF16)
        KaT = self.view(A_QK + 16384, [128, S], BF16)
        Va = self.view(A_VV, [128, NT, 65], BF16)
        b_q = Buf("QaT")
        b_k = Buf("KaT")
        b_v = Buf("Va")
        b_qz, b_kz = Buf("QaTz"), Buf("KaTz")
        self.memset(QaT[64:128], 0.0, [b_qz], eng="pool")
        self.memset(KaT[64:128], 0.0, [b_kz], eng="pool")
        self.memset(Va[:, :, 64:65], 1.0, [b_v])
        blocks = self.pre.pop(("a", g))
        for p in range(2):
            (wz, bz), (ws, bs) = blocks[2 * p], blocks[2 * p + 1]
            self.proj_fm(wz, bz, ws, bs, 128,
                         [[("rope", QaT[0:64, 2 * p, :], b_q)], [("rope", QaT[0:64, 2 * p + 1, :], b_q)]], 0)
        blocks2 = [self.load_wblock(dr["wfa"][g, 4 + i]) for i in range(2)]
        (wz, bz), (ws, bs) = blocks2
        self.proj_fm(wz, bz, ws, bs, 128, [[("rope", KaT[0:64, :], b_k)]], 0)
        for half in range(2):
            bi = 4 + half
            bk = self.banks[bi]
            for j in range(8):
                tt_ = half * 8 + j
                for k in range(8):
                    self.mm(bk[:, j * 64:(j + 1) * 64], self.xT[:, k, tt_ * 128:(tt_ + 1) * 128], self.wt[:, k, 0:64],
                            k == 0, k == 7, [self.b_xTc[tt_ // 4], self.b_wt], [self.b_bank[bi]])
            self.cp(Va[:, half * 8:(half + 1) * 8, 0:64], bk[:].rearrange("p (a b) -> p a b", b=64),
                    [self.b_bank[bi]], [b_v], eng="act")
        self.dump("QaT%d" % g, QaT[0:64], [64, 4, S], BF16, [b_q])
        self.dump("ropec", self.ropec, [128, S], F32, [self.b_rope])
        self.dump("wring0", self.wring[0], [128, 8, 128], BF16, [self.b_wring[0]])
        self.dump("wring1", self.wring[1], [128, 8, 128], BF16, [self.b_wring[1]])
        self.dump("xT", self.xT, [128, 8, S], BF16, self.b_xTc)
        self.dump("t1", self.t1, [128, 512], F32, [self.b_t1])
        self.dump("t2", self.t2, [128, 512], F32, [self.b_t2])
        self.dump("KaT%d" % g, KaT[0:64], [64, S], BF16, [b_k])
        self.dump("Va%d" % g, Va, [128, NT, 65], BF16, [b_v])
        self.init_attn_tmp()
        self.st_pairs = [0, 1]
        self.otr_bank = 7
        for qt in range(NT):
            kts = [qt - 1, qt] if qt > 0 else [qt]
            pv_i = 4 + (qt % 2)

            def score(kts=kts, qt=qt):
                p = self.next_st()
                pt, bpt = self.next_pt()
                rb = []
                for j, kt in enumerate(kts):
                    bi = 2 * p + j
                    stb = self.banks[bi]
                    self.mm(stb[:], KaT[:, kt * 128:(kt + 1) * 128], QaT[:, :, qt * 128:(qt + 1) * 128],
                            True, False, [b_k, b_q, b_qz, b_kz], [self.b_bank[bi]])
                    msk = self.dmask if kt == qt else self.umask
                    self.mm(stb[:], self.ident[:], msk[:], False, True, [self.b_const], [self.b_bank[bi]])
                    rb.append(self.b_bank[bi])
                n = len(kts)
                self.act(pt[:, 0:512 * n], self.bpair[p][:, 0:512 * n], AF.Exp, rb, [bpt], scale=0.125)
                return pt, bpt

            def pvf(tok, kts=kts, qt=qt, pv_i=pv_i):
                pt, bpt = tok
                for j, kt in enumerate(kts):
                    for h in range(4):
                        self.mm(self.banks[pv_i][:, h * 65:(h + 1) * 65],
                                pt[:, j * 512 + h * 128:j * 512 + (h + 1) * 128], Va[:, kt, :],
                                j == 0 and h == 0, (j == len(kts) - 1) and h == 3, [bpt, b_v], [self.b_bank[pv_i]],
                                skip_group_check=True)
                pv = self.banks[pv_i][:, 0:260].rearrange("p (h c) -> p h c", c=65)
                den = self.small[:, 0:4]
                self.tt(den, pv[:, :, 64], self.esink[:, 4 * g:4 * g + 4], ALU.add,
                        [self.b_bank[pv_i], self.b_const], [self.b_small])
                rden = self.small[:, 4:8]
                self.s.op("dve", lambda: self.nc.vector.reciprocal(out=rden, in_=den), [self.b_small], [self.b_small])
                ot, bot = self.next_otok()
                self.tt(ot.rearrange("p (h c) -> p h c", c=64), pv[:, :, 0:64],
                        rden.unsqueeze(2).to_broadcast([128, 4, 64]), ALU.mult,
                        [self.b_bank[pv_i], self.b_small], [bot])
                self.defer(lambda ot=ot, bot=bot, qt=qt: self.o_transpose(ot, bot, 2 * g, qt), 3)

            self.stream_add(score, pvf)
        self.stream_flush()

    def init_attn_tmp(self):
        self.pt_ring = [self.view(A_VV + 7168 + 2048 * i, [128, 1024], BF16) for i in range(3)]
        self.b_pt = [Buf("pt%d" % i) for i in range(3)]
        self.pt_i = 0
        self.otok = [self.view(A_VV + 13312 + 512 * i, [128, 256], BF16) for i in range(2)] + \
                    [self.view(A_W + 32256, [128, 256], BF16)]
        self.b_otok = [Buf("otok%d" % i) for i in range(3)]
        self.ot_i = 0
        self.small = self.view(A_VV + 15616, [128, 64], F32)
        self.b_small = Buf("small")
        self.st_i = 0
        self.st_pairs = [0, 3]
        self.stream_q = []
        self.deferred = []
        self.lookahead = 1

    def next_st(self):
        p = self.st_pairs[self.st_i % len(self.st_pairs)]
        self.st_i += 1
        return p

    def stream_add(self, score_fn, pv_fn):
        tok = score_fn()
        self.stream_q.append((pv_fn, tok))
        while len(self.stream_q) > self.lookahead:
            f, t = self.stream_q.pop(0)
            f(t)
        for d in self.deferred:
            d[0] -= 1
        while self.deferred and self.deferred[0][0] <= 0:
            self.deferred.pop(0)[1]()

    def defer(self, fn, delay, tag=None):
        self.deferred.append([delay, fn, tag])

    def force(self, tag):
        idx = [i for i, d in enumerate(self.deferred) if d[2] == tag]
        if not idx:
            return
        for _ in range(idx[-1] + 1):
            self.deferred.pop(0)[1]()

    def stream_flush(self, deferred=True):
        while self.stream_q:
            f, t = self.stream_q.pop(0)
            f(t)
        if deferred:
            while self.deferred:
                self.deferred.pop(0)[1]()

    def next_pt(self):
        i = self.pt_i % 3
        self.pt_i += 1
        return self.pt_ring[i], self.b_pt[i]

    def next_otok(self):
        i = self.ot_i % 3
        self.ot_i += 1
        return self.otok[i], self.b_otok[i]

    def o_transpose(self, ot, bot, chunk0, qt):
        ti = self.otr_bank
        tb = self.banks[ti][:].bitcast(BF16)
        for j in range(2):
            self.tr(tb[:, j * 128:(j + 1) * 128], ot[:, j * 128:(j + 1) * 128], [bot], [self.b_bank[ti]])
        self.cp(self.OT[:, chunk0:chunk0 + 2, qt * 128:(qt + 1) * 128],
                tb[:, 0:256].rearrange("p (a b) -> p a b", b=128), [self.b_bank[ti]], [self.b_OT])

    def mixer_b(self, sq, g):
        s = self.s
        dr = self.dr
        nc = self.nc
        QnT = self.view(A_QK, [128, 4, S], BF16)
        Qr = self.view(A_QK + 16384, [128, 4, S], BF16)
        Ksl = self.view(A_QK + 32768, [128, S], BF16)
        KwT = self.view(A_QK + 36864, [128, S], BF16)
        kvc = self.view(A_QK + 40960, [128, S], BF16)
        Vsl = self.view(A_VV + 2080, [128, NT, 65], BF16)
        Vw = self.view(A_VV + 4160, [128, NT, 65], BF16)
        gates = self.view(A_VV + 6240, [128, NT, 12], F32)
        b_qn, b_qr, b_ksl, b_kw, b_kvc = Buf("QnT"), Buf("Qr"), Buf("Ksl"), Buf("KwT"), Buf("kvc")
        b_vsl, b_vw, b_gates = Buf("Vsl"), Buf("Vw"), Buf("gates")
        b_aug = [Buf("aug%d" % i) for i in range(NT)]
        b_z = Buf("zpad")
        self.memset(QnT[64:128], 0.0, [b_z], eng="pool")
        self.memset(Qr[96:128], 0.0, [b_z], eng="pool")
        self.memset(Ksl[96:128], 0.0, [b_z], eng="pool")
        self.memset(KwT[64:128], 0.0, [b_z], eng="pool")
        self.memset(Vsl[:, :, 64:65], 1.0, [b_vsl])
        self.memset(Vw[:, :, 64:65], 1.0, [b_vw])
        s.dma("pool", Ksl[64:96, :], dr["eblk"], writes=[b_ksl])
        blocks = self.pre.pop(("b", g))
        if self.stop_after == "b%dp0" % g:
            return
        for p in range(2):
            (wz, bz), (ws, bs) = blocks[2 * p], blocks[2 * p + 1]
            self.proj_fm(wz, bz, ws, bs, 128,
                         [[("rope", Qr[0:64, 2 * p, :], b_qr), ("plain", QnT[0:64, 2 * p, :], b_qn)],
                          [("rope", Qr[0:64, 2 * p + 1, :], b_qr), ("plain", QnT[0:64, 2 * p + 1, :], b_qn)]], 0)
        if self.stop_after == "b%dp1" % g:
            return
        blocks2 = [self.load_wblock(dr["wfb"][g, 4 + i]) for i in range(3)]
        (wz, bz), (ws, bs) = blocks2[0], blocks2[1]
        self.proj_fm(wz, bz, ws, bs, 128, [[("rope", Ksl[0:64, :], b_ksl)], [("rope", KwT[0:64, :], b_kw)]], 0)
        if self.stop_after == "b%dp2" % g:
            return
        wz, bz = blocks2[2]
        for tc in range(4):
            bi = (tc % 2) * 2
            for k in range(8):
                self.mm(self.banks[bi][:], wz[:, k, :], self.xT[:, k, tc * 512:(tc + 1) * 512], k == 0, k == 7,
                        [bz, self.b_xTc[tc]], [self.b_bank[bi]])
            self.cp(kvc[:, tc * 512:(tc + 1) * 512], self.banks[bi][:], [self.b_bank[bi]], [b_kvc], eng="act")
        if self.stop_after == "b%dp3" % g:
            return
        for grp in range(6):
            tiles = list(range(grp * 3, min(NT, grp * 3 + 3)))
            bi = 4 + grp % 2
            bk = self.banks[bi]
            for j, tt_ in enumerate(tiles):
                for k in range(8):
                    self.mm(bk[:, j * 140:(j + 1) * 140], self.xT[:, k, tt_ * 128:(tt_ + 1) * 128], self.wt[:, k, 0:140],
                            k == 0, k == 7, [self.b_xTc[tt_ // 4], self.b_wt], [self.b_bank[bi]])
            n = len(tiles)
            bv = bk[:, 0:n * 140].rearrange("p (a b) -> p a b", b=140)
            t0 = tiles[0]
            self.cp(Vsl[:, t0:t0 + n, 0:64], bv[:, :, 0:64], [self.b_bank[bi]], [b_vsl], eng="act")
            self.cp(Vw[:, t0:t0 + n, 0:64], bv[:, :, 64:128], [self.b_bank[bi]], [b_vw], eng="dve")
            self.act(gates[:, t0:t0 + n, :], bv[:, :, 128:140], AF.Sigmoid, [self.b_bank[bi]], [b_gates])
        self.dump("QnT%d" % g, QnT[0:64], [64, 4, S], BF16, [b_qn])
        self.dump("kvc%d" % g, kvc, [128, S], BF16, [b_kvc])
        if self.stop_after == "b%dp" % g:
            return
        if not self.cbias_done:
            self.compute_cbias()
        self.init_attn_tmp()
        h1T = self.view(A_VV + 14336, [128, 2, 2, 128], BF16)
        kcmpT = self.view(A_VV + 15360, [128, 128], BF16)
        b_h1, b_kcmp = Buf("h1T"), Buf("kcmpT")
        self.memset(kcmpT[64:128], 0.0, [b_z], eng="pool")
        cva = self.cva[g]
        b_cva = Buf("cva")
        gt = self.t1
        b_gt = self.b_t1
        for kv in range(2):
            p0 = 64 * kv
            for hc in range(2):
                bi = 5 + hc
                for l in range(32):
                    self.mm(self.banks[bi][:, 0:127], self.w1kv[p0:p0 + 64, l, hc * 128:(hc + 1) * 128],
                            kvc[p0:p0 + 64, l:l + 16 * 126 + 1:16], l == 0, l == 31, [self.b_w1, b_kvc], [self.b_bank[bi]])
                xb, x2, u, sg = gt[:, 0:127], gt[:, 128:255], gt[:, 256:383], gt[:, 384:511]
                self.act(xb, self.banks[bi][:, 0:127], AF.Identity, [self.b_bank[bi], self.b_const, self.b_cbias], [b_gt],
                         bias=self.cbias[:, 2 * kv + hc:2 * kv + hc + 1], scale=1.0)
                self.tt(x2, xb, xb, ALU.mult, [b_gt], [b_gt])
                self.ts(x2, x2, 0.044715, 1.0, ALU.mult, ALU.add, [b_gt], [b_gt])
                self.tt(u, x2, xb, ALU.mult, [b_gt], [b_gt])
                self.act(sg, u, AF.Sigmoid, [b_gt], [b_gt], scale=2.0 * GELU_C)
                self.tt(h1T[:, kv, hc, 0:127], sg, xb, ALU.mult, [b_gt], [b_h1])
        for hc in range(2):
            self.mm(self.banks[5][0:64, 0:127], self.w2k[:, hc, :], h1T[:, 0, hc, 0:127], hc == 0, hc == 1,
                    [self.b_const, b_h1], [self.b_bank[5]])
        self.cp(kcmpT[0:64, 0:127], self.banks[5][0:64, 0:127], [self.b_bank[5]], [b_kcmp])
        for hc in range(2):
            self.mm(self.banks[6][0:127, 0:64], h1T[:, 1, hc, 0:127], self.w2v[:, hc, :], hc == 0, hc == 1,
                    [self.b_const, b_h1], [self.b_bank[6]])
        self.cp(cva[0:127, 0:64], self.banks[6][0:127, 0:64], [self.b_bank[6]], [b_cva])
        self.dump("kcmpT%d" % g, kcmpT[0:64], [64, 128], BF16, [b_kcmp])
        self.dump("cva%d" % g, cva[:], [128, 97], BF16, [b_cva])
        if self.stop_after == "b%dc" % g:
            return
        accs = [self.view(A_W + 27136 + 2048 + 1024 * i, [128, 256], F32) for i in range(2)]
        b_acc = [Buf("acc%d" % i) for i in range(2)]
        tmpc = self.view(A_W + 27136 + 4096 + 0, [128, 256], F32)
        b_tmpc = Buf("tmpc")
        sm = self.small
        bsm = self.b_small
        scr = self.view(A_W + 24576 + 2304, [128, 64], F32)
        b_scr = Buf("scr")
        selb = self.view(A_VV + 15872, [128, 32], BF16)
        b_selb = Buf("selb")
        selT = self.view(A_VV + 15936, [128, 32], BF16)
        b_selT = Buf("selT")

        def cmp_score(qt):
            qs = slice(qt * 128, (qt + 1) * 128)
            p = self.next_st()
            st_i = 2 * p
            stb = self.banks[st_i]
            self.mm(stb[0:127, :], kcmpT[:, 0:127], QnT[:, :, qs], True, False, [b_kcmp, b_qn, b_z], [self.b_bank[st_i]])
            self.mm(stb[0:127, :], self.ident[0:127, 0:127],
                    self.cmpbias[0:127, qs].unsqueeze(1).to_broadcast([127, 4, 128]), False, True,
                    [self.b_const], [self.b_bank[st_i]])
            pt, bpt = self.next_pt()
            self.act(pt[0:127, 0:512], stb[0:127, :], AF.Exp, [self.b_bank[st_i]], [bpt], scale=0.125)
            return pt, bpt

        def cmp_pv(tok, qt):
            pt, bpt = tok
            qs = slice(qt * 128, (qt + 1) * 128)
            for h in range(4):
                self.mm(self.banks[2][:, h * 97:(h + 1) * 97], pt[0:127, h * 128:(h + 1) * 128], cva[0:127, 0:97],
                        h == 0, h == 3, [bpt, b_cva], [self.b_bank[2]], skip_group_check=True)
            pvc = self.banks[2][:, 0:388].rearrange("p (h c) -> p h c", c=97)
            bb = self.b_bank[2]
            self.ts(sm[:, 8:12], pvc[:, :, 64], 1e-30, None, ALU.max, None, [bb], [bsm])
            self.s.op("dve", lambda: nc.vector.reciprocal(out=sm[:, 12:16], in_=sm[:, 8:12]), [bsm], [bsm])
            imp = scr[:, 0:32]
            self.ts(imp, pvc[:, 0, 65:97], sm[:, 12:13], None, ALU.mult, None, [bb, bsm], [b_scr])
            for h in range(1, 4):
                self.stt(imp, pvc[:, h, 65:97], sm[:, 12 + h:13 + h], imp, ALU.mult, ALU.add, [bb, bsm, b_scr], [b_scr])
            self.tt(imp, imp, self.sel_allowed[:, qt, :], ALU.mult, [b_scr, self.b_const], [b_scr])
            self.tt(imp, imp, self.sel_bias[:, qt, :], ALU.add, [b_scr, self.b_const], [b_scr])
            m8 = sm[:, 40:48]
            self.s.op("dve", lambda: nc.vector.max(out=m8, in_=imp), [b_scr], [bsm])
            work = scr[:, 32:64]
            self.s.op("dve", lambda: nc.vector.match_replace(out=work, in_to_replace=m8, in_values=imp, imm_value=-1e30),
                      [b_scr, bsm], [b_scr])
            m8b = sm[:, 48:56]
            self.s.op("dve", lambda: nc.vector.max(out=m8b, in_=work), [b_scr], [bsm])
            self.ts(selb, imp, sm[:, 55:56], NEG, ALU.is_lt, ALU.mult, [b_scr, bsm], [b_selb])
            self.s.op("dve", lambda: nc.vector.transpose(out=selT, in_=selb), [b_selb], [b_selT])
            for b4 in range(4):
                self.cp(Qr[64:96, :, qt * 128 + 32 * b4:qt * 128 + 32 * b4 + 32],
                        selT[32 * b4:32 * b4 + 32, :].unsqueeze(1).to_broadcast([32, 4, 32]),
                        [b_selT], [b_aug[qt]])
            self.tt(sm[:, 16:20], gates[:, qt, 0:4], sm[:, 12:16], ALU.mult, [b_gates, bsm], [bsm])
            a = accs[qt % 2]
            self.tt(a.rearrange("p (h c) -> p h c", c=64), pvc[:, :, 0:64],
                    sm[:, 16:20].unsqueeze(2).to_broadcast([128, 4, 64]), ALU.mult, [bb, bsm], [b_acc[qt % 2]])

        def branch(qt, kts, Kt, bk_k, krows, Vt, b_vt, pv_i, umask_kt, final):
            qs = slice(qt * 128, (qt + 1) * 128)
            groups = [kts[i:i + 2] for i in range(0, len(kts), 2)]
            for grp in groups:
                def score(grp=grp):
                    p = self.next_st()
                    pt, bpt = self.next_pt()
                    rb = []
                    for j, kt in enumerate(grp):
                        bi = 2 * p + j
                        stb = self.banks[bi]
                        msk = None
                        if kt == qt:
                            msk = self.dmask
                        elif kt == umask_kt:
                            msk = self.umask
                        rd = [bk_k, b_qr, b_aug[qt], b_z]
                        self.mm(stb[:], Kt[:, kt * 128:(kt + 1) * 128], Qr[:, :, qs], True, msk is None,
                                rd, [self.b_bank[bi]])
                        if msk is not None:
                            self.mm(stb[:], self.ident[:], msk[:], False, True, [self.b_const], [self.b_bank[bi]])
                        rb.append(self.b_bank[bi])
                    n = len(grp)
                    self.act(pt[:, 0:512 * n], self.bpair[p][:, 0:512 * n], AF.Exp, rb, [bpt], scale=0.125)
                    return pt, bpt

                def pvf(tok, grp=grp):
                    pt, bpt = tok
                    for j, kt in enumerate(grp):
                        for h in range(4):
                            self.mm(self.banks[pv_i][:, h * 65:(h + 1) * 65],
                                    pt[:, j * 512 + h * 128:j * 512 + (h + 1) * 128], Vt[:, kt, :],
                                    (kt == kts[0]) and h == 0, (kt == kts[-1]) and h == 3, [bpt, b_vt],
                                    [self.b_bank[pv_i]], skip_group_check=True)
                    if grp[-1] == kts[-1] and final is not None:
                        final()

                self.stream_add(score, pvf)

        def combine(qt):
            a = accs[qt % 2]
            ba = b_acc[qt % 2]
            a3 = a.rearrange("p (h c) -> p h c", c=64)
            t3 = tmpc.rearrange("p (h c) -> p h c", c=64)
            for bi_, (pv_i, c0) in enumerate([(3, 4), (4, 8)]):
                pv = self.banks[pv_i][:, 0:260].rearrange("p (h c) -> p h c", c=65)
                bb = self.b_bank[pv_i]
                rd = sm[:, 28 + 4 * bi_:32 + 4 * bi_]
                self.s.op("dve", lambda: nc.vector.reciprocal(out=rd, in_=pv[:, :, 64]), [bb], [bsm])
                fc_ = sm[:, 20 + 4 * bi_:24 + 4 * bi_]
                self.tt(fc_, gates[:, qt, c0:c0 + 4], rd, ALU.mult, [b_gates, bsm], [bsm])
                self.tt(t3, pv[:, :, 0:64], fc_.unsqueeze(2).to_broadcast([128, 4, 64]), ALU.mult, [bb, bsm], [b_tmpc])
                if bi_ == 0:
                    self.tt(a, a, tmpc, ALU.add, [ba, b_tmpc], [ba])
                else:
                    ot, bot = self.next_otok()
                    self.tt(ot, a, tmpc, ALU.add, [ba, b_tmpc], [bot])
                    self.defer(lambda ot=ot, bot=bot, qt=qt: self.o_transpose(ot, bot, 4 + 2 * g, qt), 5)

        self.st_pairs = [0, 3]
        self.otr_bank = 5
        self.stream_add(lambda: cmp_score(0), lambda tok: cmp_pv(tok, 0))
        self.stream_flush()
        if self.stop_after == "b%ds" % g:
            self.dump("Qr", Qr[0:96], [96, 4, S], BF16, [b_qr, b_aug[0]])
            self.dump("accs0", accs[0], [128, 256], F32, [b_acc[0]])
            return
        for qt in range(NT):
            if qt + 1 < NT:
                self.stream_add(lambda q=qt + 1: cmp_score(q), lambda tok, q=qt + 1: cmp_pv(tok, q))
            branch(qt, list(range(0, qt + 1)), Ksl, b_ksl, 96, Vsl, b_vsl, 3, -1, None)
            branch(qt, list(range(max(0, qt - 4), qt + 1)), KwT, b_kw, 64, Vw, b_vw, 4, qt - 4,
                   lambda q=qt: combine(q))
        self.stream_flush()

    def ln_front(self, x, bx, g_ap, b_gb, st, b_st):
        nc = self.nc
        for hf in range(2):
            self.s.op("dve", lambda: nc.vector.bn_stats(out=st[:, 6 * hf:6 * hf + 6], in_=x[:, hf * 512:(hf + 1) * 512]),
                      [bx], [b_st])
        mv = st[:, 12:14]
        self.s.op("dve", lambda: nc.vector.bn_aggr(out=mv, in_=st[:, 0:12]), [b_st], [b_st])
        rstd = st[:, 14:15]
        self.act(rstd, st[:, 13:14], AF.Sqrt, [b_st, self.b_const], [b_st], bias=self.epsc[:, 0:1], scale=1.0)
        self.s.op("dve", lambda: nc.vector.reciprocal(out=st[:, 15:16], in_=rstd), [b_st], [b_st])
        self.ts(st[:, 16:17], st[:, 12:13], st[:, 15:16], -1.0, ALU.mult, ALU.mult, [b_st], [b_st])
        self.act(x, x, AF.Identity, [bx, b_st], [bx], bias=st[:, 16:17], scale=st[:, 15:16])
        self.tt(x, x, g_ap, ALU.mult, [bx, b_gb], [bx], eng="pool")

    def ln_back(self, x, bx, b_ap, b_gb, dst, b_dst):
        self.tt(dst, x, b_ap, ALU.add, [bx, b_gb], [b_dst], eng="dve")

    def phase_c1(self, sq):
        s = self.s
        dr = self.dr
        Wpa, Wpb, b_wp, blk0 = self.pre.pop("c1")
        self.Wout = self.view(A_W + 32768, [128, 8, D], BF16)
        self.b_wout = Buf("wout")
        s.dma("pool", self.Wout, dr["wout"], writes=[self.b_wout, self.b_w1])
        sg = [self.view(A_W + 24576 + 2048 * i, [128, 512], F32) for i in range(2)]
        tm = [self.view(A_W + 28672 + 2048 * i, [128, 512], F32) for i in range(2)]
        b_sg = [Buf("sg%d" % i) for i in range(2)]
        b_tm = [Buf("tm%d" % i) for i in range(2)]
        yT = self.view(A_YT, [128, 8, S], BF16)
        self.yT = yT
        self.b_yT = Buf("yT")
        for fc in range(8):
            if fc == 0:
                w = blk0
            else:
                w = [self.load_wblock(dr["wgm"][i * 8 + fc]) for i in range(2)]
            for tc in range(4):
                ts_ = slice(tc * 512, (tc + 1) * 512)
                b0 = (tc % 2) * 4
                bk = [self.banks[b0 + i] for i in range(4)]
                bb = [self.b_bank[b0 + i] for i in range(4)]
                for k in range(4):
                    self.mm(bk[0][:], Wpa[:, k, fc * 128:(fc + 1) * 128], self.OT[:, k, ts_], k == 0, k == 3,
                            [b_wp, self.b_OT], [bb[0]])
                for k in range(4):
                    self.mm(bk[1][:], Wpb[:, k, fc * 128:(fc + 1) * 128], self.OT[:, 4 + k, ts_], k == 0, k == 3,
                            [b_wp, self.b_OT], [bb[1]])
                for i in range(2):
                    for k in range(8):
                        self.mm(bk[2 + i][:], w[i][0][:, k, :], self.xT[:, k, ts_], k == 0, k == 7,
                                [w[i][1], self.b_xTc[tc]], [bb[2 + i]])
                for i in range(2):
                    self.act(sg[i], bk[2 + i][:], AF.Sigmoid, [bb[2 + i]], [b_sg[i]])
                    self.tt(tm[i], bk[i][:], sg[i], ALU.mult, [bb[i], b_sg[i]], [b_tm[i]])
                self.tt(yT[:, fc, ts_], tm[0], tm[1], ALU.add, [b_tm[0], b_tm[1]], [self.b_yT])
        self.dump("yT", yT, [128, 8, S], BF16, [self.b_yT])

    def phase_c2(self, sq):
        s = self.s
        dr = self.dr
        Wout, b_wout = self.Wout, self.b_wout
        lng = self.view(A_W + 24576, [128, D], F32)
        lnb = self.view(A_W + 28672, [128, D], F32)
        b_ln = Buf("ln1")
        s.dma("sp", lng, dr["ln1_g"], writes=[b_ln])
        s.dma("sp", lnb, dr["ln1_b"], writes=[b_ln])
        NS = 4
        pre = [self.view(A_W + 4096 * i, [128, D], F32) for i in range(NS)]
        hbf = [self.view(A_W + 16384 + 2048 * i, [128, D], BF16) for i in range(NS)]
        st = [self.lnst[:, 32 * i:32 * i + 32] for i in range(NS)]
        b_pre = [Buf("pre%d" % i) for i in range(NS)]
        b_hbf = [Buf("hbf%d" % i) for i in range(NS)]
        b_st = [Buf("st%d" % i) for i in range(NS)]
        H = self.view(A_H, [128, NT, D], F32)
        hT = self.view(A_XT, [128, 8, S], BF16)
        self.H, self.hT = H, hT
        self.b_H = [Buf("H%d" % i) for i in range(NT)]
        self.b_hT = Buf("hT")

        def tr_tile(tt_):
            i = tt_ % NS
            tsl = slice(tt_ * 128, (tt_ + 1) * 128)
            tbi = 6 + tt_ % 2
            tb = self.banks[tbi][:].bitcast(BF16)
            for k in range(8):
                self.tr(tb[:, k * 128:(k + 1) * 128], hbf[i][:, k * 128:(k + 1) * 128], [b_hbf[i]], [self.b_bank[tbi]])
            self.cp(hT[:, :, tsl], tb.rearrange("p (a b) -> p a b", b=128), [self.b_bank[tbi]], [self.b_hT], eng="act")

        def back(tt_):
            i = tt_ % NS
            self.ln_back(pre[i], b_pre[i], lnb, b_ln, H[:, tt_, :], self.b_H[tt_])
            self.cp(hbf[i], H[:, tt_, :], [self.b_H[tt_]], [b_hbf[i]], eng="act")

        LAG = 3
        for tt_ in range(NT):
            i = tt_ % NS
            tsl = slice(tt_ * 128, (tt_ + 1) * 128)
            s.dma("sp", pre[i], dr["x"][sq, tsl, :], writes=[b_pre[i]])
            for hf in range(2):
                bi = (tt_ % 3) * 2 + hf
                for k in range(8):
                    self.mm(self.banks[bi][:], self.yT[:, k, tsl], Wout[:, k, hf * 512:(hf + 1) * 512], k == 0, k == 7,
                            [self.b_yT, b_wout], [self.b_bank[bi]])
                ph = pre[i][:, hf * 512:(hf + 1) * 512]
                self.stt(ph, ph, ALPHA, self.banks[bi][:], ALU.mult, ALU.add, [b_pre[i], self.b_bank[bi]], [b_pre[i]])
            if tt_ >= LAG:
                tr_tile(tt_ - LAG)
            self.ln_front(pre[i], b_pre[i], lng, b_ln, st[i], b_st[i])
            if tt_ >= 1:
                back(tt_ - 1)
        back(NT - 1)
        for t in range(NT - LAG, NT):
            tr_tile(t)
        self.dump("H", H, [128, NT, D], F32, self.b_H)

    def phase_d(self, sq):
        s = self.s
        dr = self.dr
        nc = self.nc
        act = self.view(A_ACT, [128, 11, S], BF16)
        b_act = Buf("act")
        ring = [self.view(A_D + 2048 * i, [128, 8, 128], BF16) for i in range(4)]
        b_ring = [Buf("ffr%d" % i) for i in range(4)]
        Wd = self.view(A_D + 8192, [128, 11, D], BF16)
        b_wd = Buf("wd")
        sgt = [self.view(A_D + 30720 + 2048 * i, [128, 512], F32) for i in range(2)]
        b_sgt = [Buf("sgt%d" % i) for i in range(2)]
        lng = self.view(A_D + 34816, [128, D], F32)
        lnb = self.view(A_D + 38912, [128, D], F32)
        b_ln = Buf("ln2")
        s.dma("sp", lng, dr["ln2_g"], writes=[b_ln])
        s.dma("sp", lnb, dr["ln2_b"], writes=[b_ln])
        st = [self.lnst[:, 32 * i:32 * i + 32] for i in range(4)]
        b_st = [Buf("lnst%d" % i) for i in range(4)]
        H, hT = self.H, self.hT

        def d_back(tt_):
            self.ln_back(H[:, tt_, :], self.b_H[tt_], lnb, b_ln, H[:, tt_, :], self.b_H[tt_])
            s.dma("sp", dr["out"][sq, tt_ * 128:(tt_ + 1) * 128, :], H[:, tt_, :], reads=[self.b_H[tt_]], writes=[])

        ri = 0
        for half in range(2):
            for hcl in range(11):
                hc = half * 11 + hcl
                w = []
                for nm in ("wg", "wu"):
                    j = ri % 4
                    ri += 1
                    s.dma("pool", ring[j], dr[nm][hc], writes=[b_ring[j]])
                    w.append((ring[j], b_ring[j]))
                if hcl == 1:
                    s.dma("pool", Wd, dr["wd"][:, half * 11:(half + 1) * 11, :], writes=[b_wd])
                for tc in range(4):
                    ts_ = slice(tc * 512, (tc + 1) * 512)
                    b0 = (tc % 2) * 2
                    for i in range(2):
                        for k in range(8):
                            self.mm(self.banks[b0 + i][:], w[i][0][:, k, :], hT[:, k, ts_], k == 0, k == 7,
                                    [w[i][1], self.b_hT], [self.b_bank[b0 + i]])
                    sgi = tc % 2
                    self.act(sgt[sgi], self.banks[b0][:], AF.Silu, [self.b_bank[b0]], [b_sgt[sgi]])
                    self.tt(act[:, hcl, ts_], sgt[sgi], self.banks[b0 + 1][:], ALU.mult,
                            [b_sgt[sgi], self.b_bank[b0 + 1]], [b_act])
            if half == 1 and sq + 1 < self.nseq and self.stop_after is None:
                self.load_xT(sq + 1, war=[self.b_hT])
            for tt_ in range(NT):
                tsl = slice(tt_ * 128, (tt_ + 1) * 128)
                if half == 0:
                    bset = 4 + (tt_ % 2) * 2
                else:
                    bset = (tt_ % 4) * 2
                for ch in range(2):
                    bi = bset + ch
                    for hcl in range(11):
                        self.mm(self.banks[bi][:], act[:, hcl, tsl], Wd[:, hcl, ch * 512:(ch + 1) * 512],
                                hcl == 0, hcl == 10, [b_act, b_wd], [self.b_bank[bi]])
                    hs = H[:, tt_, ch * 512:(ch + 1) * 512]
                    if half == 0:
                        self.stt(hs, hs, ALPHA, self.banks[bi][:], ALU.mult, ALU.add,
                                 [self.b_H[tt_], self.b_bank[bi]], [self.b_H[tt_]])
                    else:
                        self.tt(hs, hs, self.banks[bi][:], ALU.add, [self.b_H[tt_], self.b_bank[bi]], [self.b_H[tt_]])
                if half == 1:
                    self.ln_front(H[:, tt_, :], self.b_H[tt_], lng, b_ln, st[tt_ % 4], b_st[tt_ % 4])
                    if tt_ >= 1:
                        d_back(tt_ - 1)
            if half == 1:
                d_back(NT - 1)


def build_program(nseq=SEQ_PER_CORE, stop_after=None, dbg=()):
    kb = KB(nseq, stop_after, dbg)
    nc = kb.nc
    kb.lnst = None
    from contextlib import ExitStack
    with ExitStack() as es:
        kb.lnst = es.enter_context(nc.sbuf_tensor("sb_lnst", [128, 128], F32))
        kb.build()
    return kb


_CACHE = {}


def make_in_maps(inputs, ncores, nseq):
    consts = host_consts()
    wts = host_weights(inputs)
    x = np.asarray(inputs["x"], np.float32)
    maps = []
    for c in range(ncores):
        xs = np.ascontiguousarray(x[c * nseq:(c + 1) * nseq])
        m = {"x": xs, "xt": np.ascontiguousarray(xs.transpose(0, 2, 1))}
        m.update(consts)
        m.update(wts)
        maps.append(m)
    return maps


def kernel(**inputs):
    kb = build_program(SEQ_PER_CORE)
    maps = make_in_maps(inputs, NCORES, SEQ_PER_CORE)
    res = run_bass_kernel_spmd(kb.nc, maps, core_ids=list(range(NCORES)))
    out = np.concatenate([np.asarray(r["out"]) for r in res.results], axis=0)
    return out.astype(np.float32)
```

```python
import numpy as np
import concourse.bass as bass
import concourse.mybir as mybir
from concourse.bass_utils import run_bass_kernel_spmd

F32 = mybir.dt.float32
BF16 = mybir.dt.bfloat16
AF = mybir.ActivationFunctionType
ALU = mybir.AluOpType

D = 1024
S = 2048
NT = 16
DH = 64
FFN = 2816
NHC = 22
ALPHA = 2.0 ** 0.25
LN_EPS = 1e-5
NEG = -30000.0
BIG = 1e9
NCORES = 8
SEQ_PER_CORE = 4
GELU_C = 0.7978845608028654


class Buf:
    __slots__ = ("name", "w", "r", "excl")

    def __init__(self, name, excl=False):
        self.name = name
        self.w = None
        self.r = {}
        self.excl = excl


class Sched:
    ENG = ("pe", "act", "dve", "pool", "sp")

    def __init__(self, nc, sems, dma_sems):
        self.nc = nc
        self.eng = {"pe": nc.tensor, "act": nc.scalar, "dve": nc.vector, "pool": nc.gpsimd, "sp": nc.sync}
        self.sem = dict(sems)
        self.cnt = {e: 0 for e in self.ENG}
        self.seen = {e: {} for e in self.ENG}
        self.dma_keys = {q: [] for q in ("sp", "pool")}
        self.dma_val = {}
        self.dma_next = {"sp": 0, "pool": 0}
        for q, lst in dma_sems.items():
            for i, h in enumerate(lst):
                k = "d%s%d" % (q, i)
                self.sem[k] = h
                self.dma_keys[q].append(k)
                self.dma_val[k] = 0
        self.n_inst = 0

    def _wait(self, e, key, val):
        if val <= 0:
            return
        if self.seen[e].get(key, 0) >= val:
            return
        self.eng[e].wait_ge(self.sem[key], val)
        self.seen[e][key] = val

    def _deps(self, e, reads, writes):
        need = {}
        for b in reads:
            if b.w is not None:
                k, v = b.w
                if need.get(k, 0) < v:
                    need[k] = v
            if b.excl:
                for k, v in b.r.items():
                    if k != e and need.get(k, 0) < v:
                        need[k] = v
        for b in writes:
            if b.w is not None:
                k, v = b.w
                if (k != e or e != "pe") and need.get(k, 0) < v:
                    need[k] = v
            for k, v in b.r.items():
                if need.get(k, 0) < v:
                    need[k] = v
        for k, v in need.items():
            self._wait(e, k, v)

    def op(self, e, fn, reads=(), writes=()):
        self._deps(e, reads, writes)
        inst = fn()
        self.cnt[e] += 1
        c = self.cnt[e]
        inst.then_inc(self.sem[e], 1)
        self.seen[e][e] = max(self.seen[e].get(e, 0), 0)
        for b in reads:
            if b.r.get(e, 0) < c:
                b.r[e] = c
        for b in writes:
            b.w = (e, c)
            b.r = {}
        self.n_inst += 1
        return inst

    def dma(self, q, out, in_, reads=(), writes=()):
        key = self.dma_keys[q][self.dma_next[q] % len(self.dma_keys[q])]
        self.dma_next[q] += 1
        self._wait(q, key, self.dma_val[key])
        self._deps(q, reads, writes)
        self.dma_val[key] += 16
        v = self.dma_val[key]
        self.eng[q].dma_start(out=out, in_=in_).then_inc(self.sem[key], 16)
        for b in reads:
            b.r[key] = v
        for b in writes:
            b.w = (key, v)
            b.r = {}
        self.n_inst += 1

    def barrier(self, engines=None):
        engines = engines or self.ENG
        for e in self.ENG:
            for o in self.ENG:
                if o != e:
                    self._wait(e, o, self.cnt[o])
            for k, v in self.dma_val.items():
                self._wait(e, k, v)

    def final_wait(self, e="sp"):
        for k, v in self.dma_val.items():
            self._wait(e, k, v)
        for o in self.ENG:
            if o != e:
                self._wait(e, o, self.cnt[o])


COL = dict(qa=0, ka=512, va=640, qn=768, kc=1280, vc=1408, ksl=1536, vsl=1664, kw=1792, vw=1920,
           gn=2048, gm=2072)


def _blk(w):
    m = w.shape[1]
    return np.ascontiguousarray(w.reshape(8, 128, m).transpose(1, 0, 2))


def _swap(u):
    return np.concatenate([u[:, 32:64], u[:, 0:32]], axis=1)


def host_consts():
    c = {}
    c["ident"] = np.eye(128, dtype=np.float32)
    k = np.arange(128)[:, None]
    q = np.arange(128)[None, :]
    d1 = np.where(k <= q, 0.0, NEG).astype(np.float32)
    u1 = np.where(k > q, 0.0, NEG).astype(np.float32)
    c["dmask"] = np.ascontiguousarray(np.tile(d1, (1, 4)))
    c["umask"] = np.ascontiguousarray(np.tile(u1, (1, 4)))
    cc = np.arange(128)[:, None]
    t = np.arange(S)[None, :]
    c["cmpbias"] = np.where(16 * cc + 31 <= t, 0.0, NEG).astype(np.float32)
    half = DH // 2
    inv = 10000.0 ** (-np.arange(half, dtype=np.float64) / half)
    ang = np.arange(S, dtype=np.float64)[:, None] * inv[None, :]
    cos = np.cos(ang).astype(np.float32).T
    sin = np.sin(ang).astype(np.float32).T
    c["ropec"] = np.ascontiguousarray(np.concatenate([cos, cos, cos, cos], axis=0))
    c["ropes"] = np.ascontiguousarray(np.concatenate([-sin, sin, -sin, sin], axis=0))
    tt = (np.arange(NT)[None, :, None] * 128 + np.arange(128)[:, None, None])
    cur = tt // 64
    j = np.arange(32)[None, None, :]
    forced = (j == 0) | (j == cur) | (j == cur - 1)
    allowed = (~forced) & (j <= cur)
    c["sel_allowed"] = np.ascontiguousarray(np.broadcast_to(allowed, (128, NT, 32)).astype(np.float32))
    fb = np.where(forced, BIG, np.where(j <= cur, 0.0, -BIG)).astype(np.float32)
    c["sel_bias"] = np.ascontiguousarray(np.broadcast_to(fb, (128, NT, 32)).astype(np.float32))
    ncmp = 127
    c_start = np.arange(ncmp) * 16
    s_start = np.arange(32) * 64
    ov = ((c_start[:, None] < s_start[None, :] + 64) & (s_start[None, :] < c_start[:, None] + 32)).astype(np.float32)
    ova = np.zeros((128, 33), np.float32)
    ova[:ncmp, 0] = 1.0
    ova[:ncmp, 1:] = ov
    c["ovaug"] = ova
    c["eblk"] = (np.arange(S)[None, :] // 64 == np.arange(32)[:, None]).astype(np.float32)
    return c


def host_weights(inp):
    w_in = np.asarray(inp["w_in"][0], np.float32)
    out = {}

    def unit(name, idx):
        c0 = COL[name] + idx * 64
        return w_in[:, c0:c0 + 64]

    wfa = np.zeros((2, 6, 128, 8, 128), np.float32)
    wfb = np.zeros((2, 7, 128, 8, 128), np.float32)
    wta = np.zeros((2, 128, 8, 64), np.float32)
    wtb = np.zeros((2, 128, 8, 140), np.float32)
    for g in range(2):
        a1 = np.concatenate([unit("qa", 4 * g + 0), unit("qa", 4 * g + 1)], 1)
        a1s = np.concatenate([_swap(unit("qa", 4 * g + 0)), _swap(unit("qa", 4 * g + 1))], 1)
        a2 = np.concatenate([unit("qa", 4 * g + 2), unit("qa", 4 * g + 3)], 1)
        a2s = np.concatenate([_swap(unit("qa", 4 * g + 2)), _swap(unit("qa", 4 * g + 3))], 1)
        a3 = np.concatenate([unit("ka", g), unit("ka", g)], 1)
        a3s = np.concatenate([_swap(unit("ka", g)), _swap(unit("ka", g))], 1)
        for i, b in enumerate([a1, a1s, a2, a2s, a3, a3s]):
            wfa[g, i] = _blk(b)
        wta[g] = _blk(unit("va", g))
        b1 = np.concatenate([unit("qn", 4 * g + 0), unit("qn", 4 * g + 1)], 1)
        b1s = np.concatenate([_swap(unit("qn", 4 * g + 0)), _swap(unit("qn", 4 * g + 1))], 1)
        b2 = np.concatenate([unit("qn", 4 * g + 2), unit("qn", 4 * g + 3)], 1)
        b2s = np.concatenate([_swap(unit("qn", 4 * g + 2)), _swap(unit("qn", 4 * g + 3))], 1)
        b3 = np.concatenate([unit("ksl", g), unit("kw", g)], 1)
        b3s = np.concatenate([_swap(unit("ksl", g)), _swap(unit("kw", g))], 1)
        b4 = np.concatenate([unit("kc", g), unit("vc", g)], 1)
        for i, b in enumerate([b1, b1s, b2, b2s, b3, b3s, b4]):
            wfb[g, i] = _blk(b)
        gcols = [COL["gn"] + br * 8 + 4 * g + r for br in range(3) for r in range(4)]
        tb = np.concatenate([unit("vsl", g), unit("vw", g), w_in[:, gcols]], 1)
        wtb[g] = _blk(tb)
    out["wfa"] = wfa
    out["wfb"] = wfb
    out["wta"] = wta
    out["wtb"] = wtb
    gm = w_in[:, COL["gm"]:COL["gm"] + 2048]
    out["wgm"] = np.ascontiguousarray(np.stack([_blk(gm[:, i * 128:(i + 1) * 128]) for i in range(16)]))
    out["wpa"] = np.ascontiguousarray(np.asarray(inp["w_proj_a"][0], np.float32).reshape(4, 128, D).transpose(1, 0, 2))
    out["wpb"] = np.ascontiguousarray(np.asarray(inp["w_proj_b"][0], np.float32).reshape(4, 128, D).transpose(1, 0, 2))
    out["wout"] = _blk(np.asarray(inp["w_out"][0], np.float32))
    wg = np.asarray(inp["w_gate"][0], np.float32)
    wu = np.asarray(inp["w_up"][0], np.float32)
    out["wg"] = np.ascontiguousarray(np.stack([_blk(wg[:, i * 128:(i + 1) * 128]) for i in range(NHC)]))
    out["wu"] = np.ascontiguousarray(np.stack([_blk(wu[:, i * 128:(i + 1) * 128]) for i in range(NHC)]))
    out["wd"] = np.ascontiguousarray(np.asarray(inp["w_down"][0], np.float32).reshape(NHC, 128, D).transpose(1, 0, 2))
    for kv in ("k", "v"):
        w1 = np.asarray(inp["cmp_w1_" + kv][0], np.float32)
        out["w1" + kv] = np.ascontiguousarray(w1.reshape(32, 64, 256).transpose(1, 0, 2))
        w2 = np.asarray(inp["cmp_w2_" + kv][0], np.float32)
        out["w2" + kv] = np.ascontiguousarray(w2.reshape(2, 128, 64).transpose(1, 0, 2))
        b1 = np.asarray(inp["cmp_b1_" + kv][0], np.float32)
        out["b1" + kv] = np.ascontiguousarray(b1.reshape(2, 128).T)
        pe = np.asarray(inp["cmp_pe_" + kv][0], np.float32)
        out["pet" + kv] = np.ascontiguousarray(pe.T)
    out["sinks"] = np.ascontiguousarray(np.broadcast_to(np.asarray(inp["sinks"][0], np.float32)[None, :], (128, 8)))
    for nm in ("ln1_g", "ln1_b", "ln2_g", "ln2_b"):
        out[nm] = np.ascontiguousarray(np.broadcast_to(np.asarray(inp[nm][0], np.float32)[None, :], (128, D)))
    return out


A_XT = 0
A_OT = 32768
A_H = 32768
A_QK = 65536
A_VV = 114688
A_YT = 98304
A_ACT = 98304
A_W = 131072
A_D = 143360
ARENA_BYTES = 194560

CONST_SPECS = [
    ("ident", [128, 128]), ("dmask", [128, 512]), ("umask", [128, 512]), ("cmpbias", [128, S]),
    ("ropec", [128, S]), ("ropes", [128, S]), ("sel_allowed", [128, NT, 32]), ("sel_bias", [128, NT, 32]),
    ("ovaug", [128, 33]), ("eblk", [32, S]),
]
WEIGHT_SPECS = [
    ("wfa", [2, 6, 128, 8, 128]), ("wfb", [2, 7, 128, 8, 128]), ("wta", [2, 128, 8, 64]), ("wtb", [2, 128, 8, 140]),
    ("wgm", [16, 128, 8, 128]), ("wpa", [128, 4, D]), ("wpb", [128, 4, D]), ("wout", [128, 8, D]),
    ("wg", [NHC, 128, 8, 128]), ("wu", [NHC, 128, 8, 128]), ("wd", [128, NHC, D]),
    ("w1k", [64, 32, 256]), ("w1v", [64, 32, 256]), ("w2k", [128, 2, 64]), ("w2v", [128, 2, 64]),
    ("b1k", [128, 2]), ("b1v", [128, 2]), ("petk", [64, 32]), ("petv", [64, 32]), ("sinks", [128, 8]),
    ("ln1_g", [128, D]), ("ln1_b", [128, D]), ("ln2_g", [128, D]), ("ln2_b", [128, D]),
]


class KB:
    def __init__(self, nseq, stop_after=None, dbg=()):
        self.nseq = nseq
        self.stop_after = stop_after
        self.dbg = set(dbg)
        self.nc = bass.Bass("TRN2", target_bir_lowering=False)
        nc = self.nc
        self.dr = {}
        for nm, shp in CONST_SPECS + WEIGHT_SPECS:
            self.dr[nm] = nc.dram_tensor(nm, list(shp), F32, kind="ExternalInput").ap()
        self.dr["x"] = nc.dram_tensor("x", [nseq, S, D], F32, kind="ExternalInput").ap()
        self.dr["xt"] = nc.dram_tensor("xt", [nseq, D, S], F32, kind="ExternalInput").ap()
        self.dr["out"] = nc.dram_tensor("out", [nseq, S, D], F32, kind="ExternalOutput").ap()
        self.dbg_out = {}

    def view(self, off, shape, dt, parts=128, p0=0):
        esz = 4 if dt == F32 else 2
        n = int(np.prod(shape[1:]))
        assert off % 4 == 0
        ap = self.arena[p0:p0 + parts, off // 2: off // 2 + n * esz // 2]
        if dt == F32:
            ap = ap.bitcast(F32)
        if len(shape) == 3:
            ap = ap.rearrange("p (a b) -> p a b", b=shape[2])
        elif len(shape) == 4:
            ap = ap.rearrange("p (a b c) -> p a b c", b=shape[2], c=shape[3])
        return ap

    def mm(self, out, lhsT, rhs, start, stop, reads, writes, **kw):
        nc = self.nc
        return self.s.op("pe", lambda: nc.tensor.matmul(out, lhsT, rhs, start=start, stop=stop, **kw), reads, writes)

    def tr(self, out, in_, reads, writes, **kw):
        nc = self.nc
        return self.s.op("pe", lambda: nc.tensor.transpose(out, in_, self.ident[:], **kw), reads + [self.b_const], writes)

    def act(self, out, in_, func, reads, writes, **kw):
        nc = self.nc
        return self.s.op("act", lambda: nc.scalar.activation(out=out, in_=in_, func=func, **kw), reads, writes)

    def tt(self, out, in0, in1, op, reads, writes, eng="dve"):
        e = self.s.eng[eng]
        return self.s.op(eng, lambda: e.tensor_tensor(out=out, in0=in0, in1=in1, op=op), reads, writes)

    def ts(self, out, in0, s1, s2, op0, op1, reads, writes, eng="dve"):
        e = self.s.eng[eng]
        if op1 is None:
            return self.s.op(eng, lambda: e.tensor_scalar(out=out, in0=in0, scalar1=s1, scalar2=None, op0=op0), reads, writes)
        return self.s.op(eng, lambda: e.tensor_scalar(out=out, in0=in0, scalar1=s1, scalar2=s2, op0=op0, op1=op1), reads, writes)

    def stt(self, out, in0, scalar, in1, op0, op1, reads, writes):
        nc = self.nc
        return self.s.op("dve", lambda: nc.vector.scalar_tensor_tensor(out=out, in0=in0, scalar=scalar, in1=in1, op0=op0, op1=op1), reads, writes)

    def cp(self, out, in_, reads, writes, eng="dve"):
        if eng == "act":
            nc = self.nc
            return self.s.op("act", lambda: nc.scalar.copy(out=out, in_=in_), reads, writes)
        e = self.s.eng[eng]
        return self.s.op(eng, lambda: e.tensor_copy(out=out, in_=in_), reads, writes)

    def memset(self, ap, val, writes, eng="dve"):
        e = self.s.eng[eng]
        return self.s.op(eng, lambda: e.memset(ap, val), [], writes)

    def dump(self, name, ap, shape, dt, reads):
        if name not in self.dbg:
            return
        if name in self.dbg_out:
            return
        d = self.nc.dram_tensor("dbg_" + name, list(shape), dt, kind="ExternalOutput").ap()
        self.dbg_out[name] = d
        self.s.dma("sp", d, ap, reads=reads, writes=[])

    def build(self):
        nc = self.nc
        from contextlib import ExitStack
        with ExitStack() as es:
            E = es.enter_context
            self.arena = E(nc.sbuf_tensor("sb_arena", [128, ARENA_BYTES // 2], BF16))
            self.ident = E(nc.sbuf_tensor("sb_ident", [128, 128], BF16))
            self.dmask = E(nc.sbuf_tensor("sb_dmask", [128, 512], BF16))
            self.umask = E(nc.sbuf_tensor("sb_umask", [128, 512], BF16))
            self.cmpbias = E(nc.sbuf_tensor("sb_cmpbias", [128, S], BF16))
            self.sel_allowed = E(nc.sbuf_tensor("sb_sel_allowed", [128, NT, 32], F32))
            self.sel_bias = E(nc.sbuf_tensor("sb_sel_bias", [128, NT, 32], F32))
            self.cva = [E(nc.sbuf_tensor("sb_cva%d" % i, [128, 97], BF16)) for i in range(2)]
            self.w2k = E(nc.sbuf_tensor("sb_w2k", [128, 2, 64], BF16))
            self.w2v = E(nc.sbuf_tensor("sb_w2v", [128, 2, 64], BF16))
            self.cbias = E(nc.sbuf_tensor("sb_cbias", [128, 4], F32))
            self.b1 = E(nc.sbuf_tensor("sb_b1", [128, 4], F32))
            self.esink = E(nc.sbuf_tensor("sb_esink", [128, 8], F32))
            self.epsc = E(nc.sbuf_tensor("sb_epsc", [128, 1], F32))
            self.pet = E(nc.sbuf_tensor("sb_pet", [128, 32], BF16))
            self.bpair = [E(nc.psum_tensor("bpair%d" % i, [128, 1024], F32)) for i in range(4)]
            self.banks = [self.bpair[i // 2][:, (i % 2) * 512:(i % 2) * 512 + 512] for i in range(8)]
            sems = {e: E(nc.semaphore("s_" + e)) for e in Sched.ENG}
            dsems = {q: [E(nc.semaphore("d_%s%d" % (q, i))) for i in range(12)] for q in ("sp", "pool")}
            self.s = Sched(nc, sems, dsems)
            self.b_bank = [Buf("bank%d" % i, excl=True) for i in range(8)]
            self.b_const = Buf("const")
            self.b_cbias = Buf("cbias")
            self.body()
            self.s.final_wait("sp")
        return nc

    def body(self):
        s = self.s
        dr = self.dr
        bc = self.b_const
        self.load_xT(0)
        s.dma("pool", self.ident[:], dr["ident"], writes=[bc])
        s.dma("pool", self.dmask[:], dr["dmask"], writes=[bc])
        s.dma("pool", self.umask[:], dr["umask"], writes=[bc])
        s.dma("pool", self.cmpbias[:], dr["cmpbias"], writes=[bc])
        s.dma("sp", self.sel_allowed[:], dr["sel_allowed"], writes=[bc])
        s.dma("sp", self.sel_bias[:], dr["sel_bias"], writes=[bc])
        for i in range(2):
            s.dma("pool", self.cva[i][:, 64:97], dr["ovaug"], writes=[bc])
        s.dma("pool", self.w2k[:], dr["w2k"], writes=[bc])
        s.dma("pool", self.w2v[:], dr["w2v"], writes=[bc])
        s.dma("sp", self.b1[:, 0:2], dr["b1k"], writes=[bc])
        s.dma("sp", self.b1[:, 2:4], dr["b1v"], writes=[bc])
        s.dma("sp", self.esink[:], dr["sinks"], writes=[bc])
        self.act(self.esink[:], self.esink[:], AF.Exp, [bc], [bc])
        self.memset(self.epsc[:], LN_EPS, [bc])
        self.cbias_done = False
        for sq in range(self.nseq):
            self.sequence(sq)
            if self.stop_after is not None:
                break

    def compute_cbias(self):
        s = self.s
        dr = self.dr
        bc = self.b_const
        pet = self.pet
        b_pet = Buf("pet")
        s.dma("pool", pet[0:64], dr["petk"], writes=[b_pet])
        s.dma("pool", pet[64:128], dr["petv"], writes=[b_pet])
        for kv in range(2):
            p0 = 64 * kv
            for hc in range(2):
                bk = self.banks[7]
                for l in range(32):
                    self.mm(bk[:, 0:1], self.w1kv[p0:p0 + 64, l, hc * 128:(hc + 1) * 128], pet[p0:p0 + 64, l:l + 1],
                            l == 0, l == 31, [self.b_w1, b_pet], [self.b_bank[7]])
                self.tt(self.cbias[:, 2 * kv + hc:2 * kv + hc + 1], bk[:, 0:1], self.b1[:, 2 * kv + hc:2 * kv + hc + 1],
                        ALU.add, [self.b_bank[7], bc], [self.b_cbias])
        self.cbias_done = True

    def sequence(self, sq):
        s = self.s
        dr = self.dr
        self.load_xT(sq)
        self.setup_seq_bufs(sq)
        self.pre = {}
        self.pref("a", 0)
        self.w1kv = self.view(A_W + 32768, [128, 32, 256], BF16)
        self.b_w1 = Buf("w1kv")
        s.dma("pool", self.w1kv[0:64], dr["w1k"], writes=[self.b_w1])
        s.dma("pool", self.w1kv[64:128], dr["w1v"], writes=[self.b_w1])
        self.OT = self.view(A_OT, [128, 8, S], BF16)
        self.b_OT = Buf("OT")
        for g in range(2):
            self.mixer_a(sq, g)
            self.pref("b", g)
            if self.stop_after == "a%d" % g:
                self.dump("OTa", self.OT, [128, 8, S], BF16, [self.b_OT])
                return
            s.barrier()
            self.mixer_b(sq, g)
            if g == 0:
                self.pref("a", 1)
            else:
                self.pref("c1", 0)
            if self.stop_after is not None and self.stop_after.startswith("b%d" % g):
                self.dump("OTb", self.OT, [128, 8, S], BF16, [self.b_OT])
                return
            s.barrier()
        self.dump("OT", self.OT, [128, 8, S], BF16, [self.b_OT])
        if self.stop_after == "attn":
            return
        self.phase_c1(sq)
        s.barrier()
        if self.stop_after == "c1":
            return
        self.phase_c2(sq)
        s.barrier()
        if self.stop_after == "c2":
            return
        self.phase_d(sq)
        s.barrier()

    def load_xT(self, sq, war=()):
        if getattr(self, "xT_loaded", None) == sq:
            return
        self.xT_loaded = sq
        xT = self.view(A_XT, [128, 8, S], BF16)
        self.xT = xT
        self.b_xTc = [Buf("xT%d" % i) for i in range(4)]
        src = self.dr["xt"][sq].rearrange("(k p) t -> p k t", p=128)
        for tc in range(4):
            self.s.dma("pool", xT[:, :, tc * 512:(tc + 1) * 512], src[:, :, tc * 512:(tc + 1) * 512],
                       writes=[self.b_xTc[tc]] + list(war))

    def setup_seq_bufs(self, sq, war=()):
        if getattr(self, "seq_bufs_for", None) == sq:
            return
        self.seq_bufs_for = sq
        s = self.s
        self.ropec = self.view(A_W, [128, S], F32)
        self.ropes = self.view(A_W + 8192, [128, S], F32)
        self.b_rope = Buf("rope")
        s.dma("sp", self.ropec, self.dr["ropec"], writes=[self.b_rope] + list(war))
        s.dma("sp", self.ropes, self.dr["ropes"], writes=[self.b_rope] + list(war))
        self.wring = [self.view(A_W + 16384 + 2048 * i, [128, 8, 128], BF16) for i in range(4)]
        self.b_wring = [Buf("wring%d" % i) for i in range(4)]
        self.wt = self.view(A_W + 24576, [128, 8, 140], BF16)
        self.b_wt = Buf("wt")
        self.t1 = self.view(A_W + 27136, [128, 512], F32)
        self.t2 = self.view(A_W + 29184, [128, 512], F32)
        self.b_t1 = Buf("t1")
        self.b_t2 = Buf("t2")
        self.wr_i = 0

    def pref(self, kind, g):
        dr = self.dr
        if kind == "a":
            self.s.dma("pool", self.wt[:, :, 0:64], dr["wta"][g], writes=[self.b_wt])
            self.pre[(kind, g)] = [self.load_wblock(dr["wfa"][g, i]) for i in range(4)]
        elif kind == "b":
            self.s.dma("pool", self.wt[:, :, 0:140], dr["wtb"][g], writes=[self.b_wt])
            self.pre[(kind, g)] = [self.load_wblock(dr["wfb"][g, i]) for i in range(4)]
        else:
            Wpa = self.view(A_W, [128, 4, D], BF16)
            Wpb = self.view(A_W + 8192, [128, 4, D], BF16)
            b_wp = Buf("wp")
            self.s.dma("pool", Wpa, dr["wpa"], writes=[b_wp, self.b_rope])
            self.s.dma("pool", Wpb, dr["wpb"], writes=[b_wp, self.b_rope])
            blk = [self.load_wblock(dr["wgm"][i * 8]) for i in range(2)]
            self.pre["c1"] = (Wpa, Wpb, b_wp, blk)

    def load_wblock(self, src):
        i = self.wr_i % 4
        self.wr_i += 1
        self.s.dma("pool", self.wring[i], src, writes=[self.b_wring[i]])
        return self.wring[i], self.b_wring[i]

    def proj_fm(self, wz, bz, ws, bs, M, dests, bank0):
        xT = self.xT
        for tc in range(4):
            self.fm_rot = (getattr(self, "fm_rot", -1) + 1) % 3
            bz_i = (0, 2, 6)[self.fm_rot]
            bkz = self.banks[bz_i]
            bbz = self.b_bank[bz_i]
            for k in range(8):
                self.mm(bkz[0:M, :], wz[:, k, 0:M], xT[:, k, tc * 512:(tc + 1) * 512], k == 0, k == 7,
                        [bz, self.b_xTc[tc]], [bbz])
            if ws is not None:
                bks = self.banks[bz_i + 1]
                bbs = self.b_bank[bz_i + 1]
                for k in range(8):
                    self.mm(bks[0:M, :], ws[:, k, 0:M], xT[:, k, tc * 512:(tc + 1) * 512], k == 0, k == 7,
                            [bs, self.b_xTc[tc]], [bbs])
            need_rope = any(x[0] == "rope" for dl in dests for x in dl)
            if need_rope:
                self.tt(self.t1[0:M, :], bkz[0:M, :], self.ropec[0:M, tc * 512:(tc + 1) * 512], ALU.mult,
                        [bbz, self.b_rope], [self.b_t1])
                self.tt(self.t2[0:M, :], bks[0:M, :], self.ropes[0:M, tc * 512:(tc + 1) * 512], ALU.mult,
                        [bbs, self.b_rope], [self.b_t2])
            for half, dl in enumerate(dests):
                p0 = 64 * half
                for (kind, dst, bd) in dl:
                    d = dst[:, tc * 512:(tc + 1) * 512]
                    if kind == "rope":
                        self.tt(d, self.t1[p0:p0 + 64, :], self.t2[p0:p0 + 64, :], ALU.add,
                                [self.b_t1, self.b_t2], [bd])
                    else:
                        self.cp(d, bkz[p0:p0 + 64, :], [bbz], [bd], eng=("act" if p0 == 0 else "dve"))

    def mixer_a(self, sq, g):
        s = self.s
        dr = self.dr
        QaT = self.view(A_QK, [128, 4, S], BF16)
        KaT = self.view(A_QK + 16384, [128, S], BF16)
        Va = self.view(A_VV, [128, NT, 65], BF16)
        b_q = Buf("QaT")
        b_k = Buf("KaT")
        b_v = Buf("Va")
        b_qz, b_kz = Buf("QaTz"), Buf("KaTz")
        self.memset(QaT[64:128], 0.0, [b_qz], eng="pool")
        self.memset(KaT[64:128], 0.0, [b_kz], eng="pool")
        self.memset(Va[:, :, 64:65], 1.0, [b_v])
        blocks = self.pre.pop(("a", g))
        for p in range(2):
            (wz, bz), (ws, bs) = blocks[2 * p], blocks[2 * p + 1]
            self.proj_fm(wz, bz, ws, bs, 128,
                         [[("rope", QaT[0:64, 2 * p, :], b_q)], [("rope", QaT[0:64, 2 * p + 1, :], b_q)]], 0)
        blocks2 = [self.load_wblock(dr["wfa"][g, 4 + i]) for i in range(2)]
        (wz, bz), (ws, bs) = blocks2
        self.proj_fm(wz, bz, ws, bs, 128, [[("rope", KaT[0:64, :], b_k)]], 0)
        for half in range(2):
            bi = 4 + half
            bk = self.banks[bi]
            for j in range(8):
                tt_ = half * 8 + j
                for k in range(8):
                    self.mm(bk[:, j * 64:(j + 1) * 64], self.xT[:, k, tt_ * 128:(tt_ + 1) * 128], self.wt[:, k, 0:64],
                            k == 0, k == 7, [self.b_xTc[tt_ // 4], self.b_wt], [self.b_bank[bi]])
            self.cp(Va[:, half * 8:(half + 1) * 8, 0:64], bk[:].rearrange("p (a b) -> p a b", b=64),
                    [self.b_bank[bi]], [b_v], eng="act")
        self.dump("QaT%d" % g, QaT[0:64], [64, 4, S], BF16, [b_q])
        self.dump("ropec", self.ropec, [128, S], F32, [self.b_rope])
        self.dump("wring0", self.wring[0], [128, 8, 128], BF16, [self.b_wring[0]])
        self.dump("wring1", self.wring[1], [128, 8, 128], BF16, [self.b_wring[1]])
        self.dump("xT", self.xT, [128, 8, S], BF16, self.b_xTc)
        self.dump("t1", self.t1, [128, 512], F32, [self.b_t1])
        self.dump("t2", self.t2, [128, 512], F32, [self.b_t2])
        self.dump("KaT%d" % g, KaT[0:64], [64, S], BF16, [b_k])
        self.dump("Va%d" % g, Va, [128, NT, 65], BF16, [b_v])
        self.init_attn_tmp()
        self.st_pairs = [0, 1]
        self.otr_bank = 7
        for qt in range(NT):
            kts = [qt - 1, qt] if qt > 0 else [qt]
            pv_i = 4 + (qt % 2)

            def score(kts=kts, qt=qt):
                p = self.next_st()
                pt, bpt = self.next_pt()
                rb = []
                for j, kt in enumerate(kts):
                    bi = 2 * p + j
                    stb = self.banks[bi]
                    self.mm(stb[:], KaT[:, kt * 128:(kt + 1) * 128], QaT[:, :, qt * 128:(qt + 1) * 128],
                            True, False, [b_k, b_q, b_qz, b_kz], [self.b_bank[bi]])
                    msk = self.dmask if kt == qt else self.umask
                    self.mm(stb[:], self.ident[:], msk[:], False, True, [self.b_const], [self.b_bank[bi]])
                    rb.append(self.b_bank[bi])
                n = len(kts)
                self.act(pt[:, 0:512 * n], self.bpair[p][:, 0:512 * n], AF.Exp, rb, [bpt], scale=0.125)
                return pt, bpt

            def pvf(tok, kts=kts, qt=qt, pv_i=pv_i):
                pt, bpt = tok
                for j, kt in enumerate(kts):
                    for h in range(4):
                        self.mm(self.banks[pv_i][:, h * 65:(h + 1) * 65],
                                pt[:, j * 512 + h * 128:j * 512 + (h + 1) * 128], Va[:, kt, :],
                                j == 0 and h == 0, (j == len(kts) - 1) and h == 3, [bpt, b_v], [self.b_bank[pv_i]],
                                skip_group_check=True)
                pv = self.banks[pv_i][:, 0:260].rearrange("p (h c) -> p h c", c=65)
                den = self.small[:, 0:4]
                self.tt(den, pv[:, :, 64], self.esink[:, 4 * g:4 * g + 4], ALU.add,
                        [self.b_bank[pv_i], self.b_const], [self.b_small])
                rden = self.small[:, 4:8]
                self.s.op("dve", lambda: self.nc.vector.reciprocal(out=rden, in_=den), [self.b_small], [self.b_small])
                ot, bot = self.next_otok()
                self.tt(ot.rearrange("p (h c) -> p h c", c=64), pv[:, :, 0:64],
                        rden.unsqueeze(2).to_broadcast([128, 4, 64]), ALU.mult,
                        [self.b_bank[pv_i], self.b_small], [bot])
                self.defer(lambda ot=ot, bot=bot, qt=qt: self.o_transpose(ot, bot, 2 * g, qt), 3)

            self.stream_add(score, pvf)
        self.stream_flush()

    def init_attn_tmp(self):
        self.pt_ring = [self.view(A_VV + 7168 + 2048 * i, [128, 1024], BF16) for i in range(3)]
        self.b_pt = [Buf("pt%d" % i) for i in range(3)]
        self.pt_i = 0
        self.otok = [self.view(A_VV + 13312 + 512 * i, [128, 256], BF16) for i in range(2)] + \
                    [self.view(A_W + 32256, [128, 256], BF16)]
        self.b_otok = [Buf("otok%d" % i) for i in range(3)]
        self.ot_i = 0
        self.small = self.view(A_VV + 15616, [128, 64], F32)
        self.b_small = Buf("small")
        self.st_i = 0
        self.st_pairs = [0, 3]
        self.stream_q = []
        self.deferred = []
        self.lookahead = 1

    def next_st(self):
        p = self.st_pairs[self.st_i % len(self.st_pairs)]
        self.st_i += 1
        return p

    def stream_add(self, score_fn, pv_fn):
        tok = score_fn()
        self.stream_q.append((pv_fn, tok))
        while len(self.stream_q) > self.lookahead:
            f, t = self.stream_q.pop(0)
            f(t)
        for d in self.deferred:
            d[0] -= 1
        while self.deferred and self.deferred[0][0] <= 0:
            self.deferred.pop(0)[1]()

    def defer(self, fn, delay, tag=None):
        self.deferred.append([delay, fn, tag])

    def force(self, tag):
        idx = [i for i, d in enumerate(self.deferred) if d[2] == tag]
        if not idx:
            return
        for _ in range(idx[-1] + 1):
            self.deferred.pop(0)[1]()

    def stream_flush(self, deferred=True):
        while self.stream_q:
            f, t = self.stream_q.pop(0)
            f(t)
        if deferred:
            while self.deferred:
                self.deferred.pop(0)[1]()

    def next_pt(self):
        i = self.pt_i % 3
        self.pt_i += 1
        return self.pt_ring[i], self.b_pt[i]

    def next_otok(self):
        i = self.ot_i % 3
        self.ot_i += 1
        return self.otok[i], self.b_otok[i]

    def o_transpose(self, ot, bot, chunk0, qt):
        ti = self.otr_bank
        tb = self.banks[ti][:].bitcast(BF16)
        for j in range(2):
            self.tr(tb[:, j * 128:(j + 1) * 128], ot[:, j * 128:(j + 1) * 128], [bot], [self.b_bank[ti]])
        self.cp(self.OT[:, chunk0:chunk0 + 2, qt * 128:(qt + 1) * 128],
                tb[:, 0:256].rearrange("p (a b) -> p a b", b=128), [self.b_bank[ti]], [self.b_OT])

    def mixer_b(self, sq, g):
        s = self.s
        dr = self.dr
        nc = self.nc
        QnT = self.view(A_QK, [128, 4, S], BF16)
        Qr = self.view(A_QK + 16384, [128, 4, S], BF16)
        Ksl = self.view(A_QK + 32768, [128, S], BF16)
        KwT = self.view(A_QK + 36864, [128, S], BF16)
        kvc = self.view(A_QK + 40960, [128, S], BF16)
        Vsl = self.view(A_VV + 2080, [128, NT, 65], BF16)
        Vw = self.view(A_VV + 4160, [128, NT, 65], BF16)
        gates = self.view(A_VV + 6240, [128, NT, 12], F32)
        b_qn, b_qr, b_ksl, b_kw, b_kvc = Buf("QnT"), Buf("Qr"), Buf("Ksl"), Buf("KwT"), Buf("kvc")
        b_vsl, b_vw, b_gates = Buf("Vsl"), Buf("Vw"), Buf("gates")
        b_aug = [Buf("aug%d" % i) for i in range(NT)]
        b_z = Buf("zpad")
        self.memset(QnT[64:128], 0.0, [b_z], eng="pool")
        self.memset(Qr[96:128], 0.0, [b_z], eng="pool")
        self.memset(Ksl[96:128], 0.0, [b_z], eng="pool")
        self.memset(KwT[64:128], 0.0, [b_z], eng="pool")
        self.memset(Vsl[:, :, 64:65], 1.0, [b_vsl])
        self.memset(Vw[:, :, 64:65], 1.0, [b_vw])
        s.dma("pool", Ksl[64:96, :], dr["eblk"], writes=[b_ksl])
        blocks = self.pre.pop(("b", g))
        if self.stop_after == "b%dp0" % g:
            return
        for p in range(2):
            (wz, bz), (ws, bs) = blocks[2 * p], blocks[2 * p + 1]
            self.proj_fm(wz, bz, ws, bs, 128,
                         [[("rope", Qr[0:64, 2 * p, :], b_qr), ("plain", QnT[0:64, 2 * p, :], b_qn)],
                          [("rope", Qr[0:64, 2 * p + 1, :], b_qr), ("plain", QnT[0:64, 2 * p + 1, :], b_qn)]], 0)
        if self.stop_after == "b%dp1" % g:
            return
        blocks2 = [self.load_wblock(dr["wfb"][g, 4 + i]) for i in range(3)]
        (wz, bz), (ws, bs) = blocks2[0], blocks2[1]
        self.proj_fm(wz, bz, ws, bs, 128, [[("rope", Ksl[0:64, :], b_ksl)], [("rope", KwT[0:64, :], b_kw)]], 0)
        if self.stop_after == "b%dp2" % g:
            return
        wz, bz = blocks2[2]
        for tc in range(4):
            bi = (tc % 2) * 2
            for k in range(8):
                self.mm(self.banks[bi][:], wz[:, k, :], self.xT[:, k, tc * 512:(tc + 1) * 512], k == 0, k == 7,
                        [bz, self.b_xTc[tc]], [self.b_bank[bi]])
            self.cp(kvc[:, tc * 512:(tc + 1) * 512], self.banks[bi][:], [self.b_bank[bi]], [b_kvc], eng="act")
        if self.stop_after == "b%dp3" % g:
            return
        for grp in range(6):
            tiles = list(range(grp * 3, min(NT, grp * 3 + 3)))
            bi = 4 + grp % 2
            bk = self.banks[bi]
            for j, tt_ in enumerate(tiles):
                for k in range(8):
                    self.mm(bk[:, j * 140:(j + 1) * 140], self.xT[:, k, tt_ * 128:(tt_ + 1) * 128], self.wt[:, k, 0:140],
                            k == 0, k == 7, [self.b_xTc[tt_ // 4], self.b_wt], [self.b_bank[bi]])
            n = len(tiles)
            bv = bk[:, 0:n * 140].rearrange("p (a b) -> p a b", b=140)
            t0 = tiles[0]
            self.cp(Vsl[:, t0:t0 + n, 0:64], bv[:, :, 0:64], [self.b_bank[bi]], [b_vsl], eng="act")
            self.cp(Vw[:, t0:t0 + n, 0:64], bv[:, :, 64:128], [self.b_bank[bi]], [b_vw], eng="dve")
            self.act(gates[:, t0:t0 + n, :], bv[:, :, 128:140], AF.Sigmoid, [self.b_bank[bi]], [b_gates])
        self.dump("QnT%d" % g, QnT[0:64], [64, 4, S], BF16, [b_qn])
        self.dump("kvc%d" % g, kvc, [128, S], BF16, [b_kvc])
        if self.stop_after == "b%dp" % g:
            return
        if not self.cbias_done:
            self.compute_cbias()
        self.init_attn_tmp()
        h1T = self.view(A_VV + 14336, [128, 2, 2, 128], BF16)
        kcmpT = self.view(A_VV + 15360, [128, 128], BF16)
        b_h1, b_kcmp = Buf("h1T"), Buf("kcmpT")
        self.memset(kcmpT[64:128], 0.0, [b_z], eng="pool")
        cva = self.cva[g]
        b_cva = Buf("cva")
        gts = [(self.t1, self.b_t1), (self.t2, self.b_t2)]
        for kv in range(2):
            p0 = 64 * kv
            for hc in range(2):
                bi = 5 + hc
                gt, b_gt = gts[hc]
                for l in range(32):
                    self.mm(self.banks[bi][:, 0:127], self.w1kv[p0:p0 + 64, l, hc * 128:(hc + 1) * 128],
                            kvc[p0:p0 + 64, l:l + 16 * 126 + 1:16], l == 0, l == 31, [self.b_w1, b_kvc], [self.b_bank[bi]])
                xb, x2, u, sg = gt[:, 0:127], gt[:, 128:255], gt[:, 256:383], gt[:, 384:511]
                self.act(xb, self.banks[bi][:, 0:127], AF.Identity, [self.b_bank[bi], self.b_const, self.b_cbias], [b_gt],
                         bias=self.cbias[:, 2 * kv + hc:2 * kv + hc + 1], scale=1.0)
                self.tt(x2, xb, xb, ALU.mult, [b_gt], [b_gt])
                self.ts(x2, x2, 0.044715, 1.0, ALU.mult, ALU.add, [b_gt], [b_gt])
                self.tt(u, x2, xb, ALU.mult, [b_gt], [b_gt])
                self.act(sg, u, AF.Sigmoid, [b_gt], [b_gt], scale=2.0 * GELU_C)
                self.tt(h1T[:, kv, hc, 0:127], sg, xb, ALU.mult, [b_gt], [b_h1])
        for hc in range(2):
            self.mm(self.banks[5][0:64, 0:127], self.w2k[:, hc, :], h1T[:, 0, hc, 0:127], hc == 0, hc == 1,
                    [self.b_const, b_h1], [self.b_bank[5]])
        self.cp(kcmpT[0:64, 0:127], self.banks[5][0:64, 0:127], [self.b_bank[5]], [b_kcmp])
        for hc in range(2):
            self.mm(self.banks[6][0:127, 0:64], h1T[:, 1, hc, 0:127], self.w2v[:, hc, :], hc == 0, hc == 1,
                    [self.b_const, b_h1], [self.b_bank[6]])
        self.cp(cva[0:127, 0:64], self.banks[6][0:127, 0:64], [self.b_bank[6]], [b_cva])
        self.dump("kcmpT%d" % g, kcmpT[0:64], [64, 128], BF16, [b_kcmp])
        self.dump("cva%d" % g, cva[:], [128, 97], BF16, [b_cva])
        if self.stop_after == "b%dc" % g:
            return
        accs = [self.view(A_W + 27136 + 2048 + 1024 * i, [128, 256], F32) for i in range(2)]
        b_acc = [Buf("acc%d" % i) for i in range(2)]
        tmpc = self.view(A_W + 27136 + 4096 + 0, [128, 256], F32)
        b_tmpc = Buf("tmpc")
        sm = self.small
        bsm = self.b_small
        scr = self.view(A_W + 24576 + 2304, [128, 64], F32)
        b_scr = Buf("scr")
        selb = self.view(A_VV + 15872, [128, 32], BF16)
        b_selb = Buf("selb")
        selT = self.view(A_VV + 15936, [128, 32], BF16)
        b_selT = Buf("selT")

        def cmp_score(qt):
            qs = slice(qt * 128, (qt + 1) * 128)
            p = self.next_st()
            st_i = 2 * p
            stb = self.banks[st_i]
            self.mm(stb[0:127, :], kcmpT[:, 0:127], QnT[:, :, qs], True, False, [b_kcmp, b_qn, b_z], [self.b_bank[st_i]])
            self.mm(stb[0:127, :], self.ident[0:127, 0:127],
                    self.cmpbias[0:127, qs].unsqueeze(1).to_broadcast([127, 4, 128]), False, True,
                    [self.b_const], [self.b_bank[st_i]])
            pt, bpt = self.next_pt()
            self.act(pt[0:127, 0:512], stb[0:127, :], AF.Exp, [self.b_bank[st_i]], [bpt], scale=0.125)
            return pt, bpt

        def cmp_pv(tok, qt):
            pt, bpt = tok
            qs = slice(qt * 128, (qt + 1) * 128)
            for h in range(4):
                self.mm(self.banks[2][:, h * 97:(h + 1) * 97], pt[0:127, h * 128:(h + 1) * 128], cva[0:127, 0:97],
                        h == 0, h == 3, [bpt, b_cva], [self.b_bank[2]], skip_group_check=True)
            pvc = self.banks[2][:, 0:388].rearrange("p (h c) -> p h c", c=97)
            bb = self.b_bank[2]
            self.ts(sm[:, 8:12], pvc[:, :, 64], 1e-30, None, ALU.max, None, [bb], [bsm])
            self.s.op("dve", lambda: nc.vector.reciprocal(out=sm[:, 12:16], in_=sm[:, 8:12]), [bsm], [bsm])
            imp = scr[:, 0:32]
            self.ts(imp, pvc[:, 0, 65:97], sm[:, 12:13], None, ALU.mult, None, [bb, bsm], [b_scr])
            for h in range(1, 4):
                self.stt(imp, pvc[:, h, 65:97], sm[:, 12 + h:13 + h], imp, ALU.mult, ALU.add, [bb, bsm, b_scr], [b_scr])
            self.tt(imp, imp, self.sel_allowed[:, qt, :], ALU.mult, [b_scr, self.b_const], [b_scr])
            self.tt(imp, imp, self.sel_bias[:, qt, :], ALU.add, [b_scr, self.b_const], [b_scr])
            m8 = sm[:, 40:48]
            self.s.op("dve", lambda: nc.vector.max(out=m8, in_=imp), [b_scr], [bsm])
            work = scr[:, 32:64]
            self.s.op("dve", lambda: nc.vector.match_replace(out=work, in_to_replace=m8, in_values=imp, imm_value=-1e30),
                      [b_scr, bsm], [b_scr])
            m8b = sm[:, 48:56]
            self.s.op("dve", lambda: nc.vector.max(out=m8b, in_=work), [b_scr], [bsm])
            self.ts(selb, imp, sm[:, 55:56], NEG, ALU.is_lt, ALU.mult, [b_scr, bsm], [b_selb])
            self.s.op("dve", lambda: nc.vector.transpose(out=selT, in_=selb), [b_selb], [b_selT])
            for b4 in range(4):
                self.cp(Qr[64:96, :, qt * 128 + 32 * b4:qt * 128 + 32 * b4 + 32],
                        selT[32 * b4:32 * b4 + 32, :].unsqueeze(1).to_broadcast([32, 4, 32]),
                        [b_selT], [b_aug[qt]])
            self.tt(sm[:, 16:20], gates[:, qt, 0:4], sm[:, 12:16], ALU.mult, [b_gates, bsm], [bsm])
            a = accs[qt % 2]
            self.tt(a.rearrange("p (h c) -> p h c", c=64), pvc[:, :, 0:64],
                    sm[:, 16:20].unsqueeze(2).to_broadcast([128, 4, 64]), ALU.mult, [bb, bsm], [b_acc[qt % 2]])

        def branch(qt, kts, Kt, bk_k, krows, Vt, b_vt, pv_i, umask_kt, final):
            qs = slice(qt * 128, (qt + 1) * 128)
            groups = [kts[i:i + 2] for i in range(0, len(kts), 2)]
            for grp in groups:
                def score(grp=grp):
                    p = self.next_st()
                    pt, bpt = self.next_pt()
                    rb = []
                    for j, kt in enumerate(grp):
                        bi = 2 * p + j
                        stb = self.banks[bi]
                        msk = None
                        if kt == qt:
                            msk = self.dmask
                        elif kt == umask_kt:
                            msk = self.umask
                        rd = [bk_k, b_qr, b_aug[qt], b_z]
                        self.mm(stb[:], Kt[:, kt * 128:(kt + 1) * 128], Qr[:, :, qs], True, msk is None,
                                rd, [self.b_bank[bi]])
                        if msk is not None:
                            self.mm(stb[:], self.ident[:], msk[:], False, True, [self.b_const], [self.b_bank[bi]])
                        rb.append(self.b_bank[bi])
                    n = len(grp)
                    self.act(pt[:, 0:512 * n], self.bpair[p][:, 0:512 * n], AF.Exp, rb, [bpt], scale=0.125)
                    return pt, bpt

                def pvf(tok, grp=grp):
                    pt, bpt = tok
                    for j, kt in enumerate(grp):
                        for h in range(4):
                            self.mm(self.banks[pv_i][:, h * 65:(h + 1) * 65],
                                    pt[:, j * 512 + h * 128:j * 512 + (h + 1) * 128], Vt[:, kt, :],
                                    (kt == kts[0]) and h == 0, (kt == kts[-1]) and h == 3, [bpt, b_vt],
                                    [self.b_bank[pv_i]], skip_group_check=True)
                    if grp[-1] == kts[-1] and final is not None:
                        final()

                self.stream_add(score, pvf)

        def combine(qt):
            a = accs[qt % 2]
            ba = b_acc[qt % 2]
            a3 = a.rearrange("p (h c) -> p h c", c=64)
            t3 = tmpc.rearrange("p (h c) -> p h c", c=64)
            for bi_, (pv_i, c0) in enumerate([(3, 4), (4, 8)]):
                pv = self.banks[pv_i][:, 0:260].rearrange("p (h c) -> p h c", c=65)
                bb = self.b_bank[pv_i]
                rd = sm[:, 28 + 4 * bi_:32 + 4 * bi_]
                self.s.op("dve", lambda: nc.vector.reciprocal(out=rd, in_=pv[:, :, 64]), [bb], [bsm])
                fc_ = sm[:, 20 + 4 * bi_:24 + 4 * bi_]
                self.tt(fc_, gates[:, qt, c0:c0 + 4], rd, ALU.mult, [b_gates, bsm], [bsm])
                self.tt(t3, pv[:, :, 0:64], fc_.unsqueeze(2).to_broadcast([128, 4, 64]), ALU.mult, [bb, bsm], [b_tmpc])
                if bi_ == 0:
                    self.tt(a, a, tmpc, ALU.add, [ba, b_tmpc], [ba])
                else:
                    ot, bot = self.next_otok()
                    self.tt(ot, a, tmpc, ALU.add, [ba, b_tmpc], [bot])
                    self.defer(lambda ot=ot, bot=bot, qt=qt: self.o_transpose(ot, bot, 4 + 2 * g, qt), 5)

        self.st_pairs = [0, 3]
        self.otr_bank = 5
        self.stream_add(lambda: cmp_score(0), lambda tok: cmp_pv(tok, 0))
        self.stream_flush()
        if self.stop_after == "b%ds" % g:
            self.dump("Qr", Qr[0:96], [96, 4, S], BF16, [b_qr, b_aug[0]])
            self.dump("accs0", accs[0], [128, 256], F32, [b_acc[0]])
            return
        for qt in range(NT):
            if qt + 1 < NT:
                self.stream_add(lambda q=qt + 1: cmp_score(q), lambda tok, q=qt + 1: cmp_pv(tok, q))
            branch(qt, list(range(0, qt + 1)), Ksl, b_ksl, 96, Vsl, b_vsl, 3, -1, None)
            branch(qt, list(range(max(0, qt - 4), qt + 1)), KwT, b_kw, 64, Vw, b_vw, 4, qt - 4,
                   lambda q=qt: combine(q))
        self.stream_flush()

    def ln_front(self, x, bx, g_ap, b_gb, st, b_st):
        nc = self.nc
        for hf in range(2):
            self.s.op("dve", lambda: nc.vector.bn_stats(out=st[:, 6 * hf:6 * hf + 6], in_=x[:, hf * 512:(hf + 1) * 512]),
                      [bx], [b_st])
        mv = st[:, 12:14]
        self.s.op("dve", lambda: nc.vector.bn_aggr(out=mv, in_=st[:, 0:12]), [b_st], [b_st])
        rstd = st[:, 14:15]
        self.act(rstd, st[:, 13:14], AF.Sqrt, [b_st, self.b_const], [b_st], bias=self.epsc[:, 0:1], scale=1.0)
        self.s.op("dve", lambda: nc.vector.reciprocal(out=st[:, 15:16], in_=rstd), [b_st], [b_st])
        self.ts(st[:, 16:17], st[:, 12:13], st[:, 15:16], -1.0, ALU.mult, ALU.mult, [b_st], [b_st])
        self.act(x, x, AF.Identity, [bx, b_st], [bx], bias=st[:, 16:17], scale=st[:, 15:16])
        self.tt(x, x, g_ap, ALU.mult, [bx, b_gb], [bx], eng="pool")

    def ln_back(self, x, bx, b_ap, b_gb, dst, b_dst):
        self.tt(dst, x, b_ap, ALU.add, [bx, b_gb], [b_dst], eng="dve")

    def phase_c1(self, sq):
        s = self.s
        dr = self.dr
        Wpa, Wpb, b_wp, blk0 = self.pre.pop("c1")
        self.Wout = self.view(A_W + 32768, [128, 8, D], BF16)
        self.b_wout = Buf("wout")
        s.dma("pool", self.Wout, dr["wout"], writes=[self.b_wout, self.b_w1])
        sg = [self.view(A_W + 24576 + 2048 * i, [128, 512], F32) for i in range(2)]
        tm = [self.view(A_W + 28672 + 2048 * i, [128, 512], F32) for i in range(2)]
        b_sg = [Buf("sg%d" % i) for i in range(2)]
        b_tm = [Buf("tm%d" % i) for i in range(2)]
        yT = self.view(A_YT, [128, 8, S], BF16)
        self.yT = yT
        self.b_yT = Buf("yT")
        for fc in range(8):
            if fc == 0:
                w = blk0
            else:
                w = [self.load_wblock(dr["wgm"][i * 8 + fc]) for i in range(2)]
            for tc in range(4):
                ts_ = slice(tc * 512, (tc + 1) * 512)
                b0 = (tc % 2) * 4
                bk = [self.banks[b0 + i] for i in range(4)]
                bb = [self.b_bank[b0 + i] for i in range(4)]
                for k in range(4):
                    self.mm(bk[0][:], Wpa[:, k, fc * 128:(fc + 1) * 128], self.OT[:, k, ts_], k == 0, k == 3,
                            [b_wp, self.b_OT], [bb[0]])
                for k in range(4):
                    self.mm(bk[1][:], Wpb[:, k, fc * 128:(fc + 1) * 128], self.OT[:, 4 + k, ts_], k == 0, k == 3,
                            [b_wp, self.b_OT], [bb[1]])
                for i in range(2):
                    for k in range(8):
                        self.mm(bk[2 + i][:], w[i][0][:, k, :], self.xT[:, k, ts_], k == 0, k == 7,
                                [w[i][1], self.b_xTc[tc]], [bb[2 + i]])
                for i in range(2):
                    self.act(sg[i], bk[2 + i][:], AF.Sigmoid, [bb[2 + i]], [b_sg[i]])
                    self.tt(tm[i], bk[i][:], sg[i], ALU.mult, [bb[i], b_sg[i]], [b_tm[i]])
                self.tt(yT[:, fc, ts_], tm[0], tm[1], ALU.add, [b_tm[0], b_tm[1]], [self.b_yT])
        self.dump("yT", yT, [128, 8, S], BF16, [self.b_yT])

    def phase_c2(self, sq):
        s = self.s
        dr = self.dr
        Wout, b_wout = self.Wout, self.b_wout
        lng = self.view(A_W + 24576, [128, D], F32)
        lnb = self.view(A_W + 28672, [128, D], F32)
        b_ln = Buf("ln1")
        s.dma("sp", lng, dr["ln1_g"], writes=[b_ln])
        s.dma("sp", lnb, dr["ln1_b"], writes=[b_ln])
        NS = 4
        pre = [self.view(A_W + 4096 * i, [128, D], F32) for i in range(NS)]
        hbf = [self.view(A_W + 16384 + 2048 * i, [128, D], BF16) for i in range(NS)]
        st = [self.lnst[:, 32 * i:32 * i + 32] for i in range(NS)]
        b_pre = [Buf("pre%d" % i) for i in range(NS)]
        b_hbf = [Buf("hbf%d" % i) for i in range(NS)]
        b_st = [Buf("st%d" % i) for i in range(NS)]
        H = self.view(A_H, [128, NT, D], F32)
        hT = self.view(A_XT, [128, 8, S], BF16)
        self.H, self.hT = H, hT
        self.b_H = [Buf("H%d" % i) for i in range(NT)]
        self.b_hT = Buf("hT")

        def tr_tile(tt_):
            i = tt_ % NS
            tsl = slice(tt_ * 128, (tt_ + 1) * 128)
            tbi = 6 + tt_ % 2
            tb = self.banks[tbi][:].bitcast(BF16)
            for k in range(8):
                self.tr(tb[:, k * 128:(k + 1) * 128], hbf[i][:, k * 128:(k + 1) * 128], [b_hbf[i]], [self.b_bank[tbi]])
            self.cp(hT[:, :, tsl], tb.rearrange("p (a b) -> p a b", b=128), [self.b_bank[tbi]], [self.b_hT], eng="act")

        def back(tt_):
            i = tt_ % NS
            self.ln_back(pre[i], b_pre[i], lnb, b_ln, H[:, tt_, :], self.b_H[tt_])
            self.cp(hbf[i], H[:, tt_, :], [self.b_H[tt_]], [b_hbf[i]], eng="act")

        LAG = 3
        for tt_ in range(NT):
            i = tt_ % NS
            tsl = slice(tt_ * 128, (tt_ + 1) * 128)
            s.dma("sp", pre[i], dr["x"][sq, tsl, :], writes=[b_pre[i]])
            for hf in range(2):
                bi = (tt_ % 3) * 2 + hf
                for k in range(8):
                    self.mm(self.banks[bi][:], self.yT[:, k, tsl], Wout[:, k, hf * 512:(hf + 1) * 512], k == 0, k == 7,
                            [self.b_yT, b_wout], [self.b_bank[bi]])
                ph = pre[i][:, hf * 512:(hf + 1) * 512]
                self.stt(ph, ph, ALPHA, self.banks[bi][:], ALU.mult, ALU.add, [b_pre[i], self.b_bank[bi]], [b_pre[i]])
            if tt_ >= LAG:
                tr_tile(tt_ - LAG)
            self.ln_front(pre[i], b_pre[i], lng, b_ln, st[i], b_st[i])
            if tt_ >= 1:
                back(tt_ - 1)
        back(NT - 1)
        for t in range(NT - LAG, NT):
            tr_tile(t)
        self.dump("H", H, [128, NT, D], F32, self.b_H)

    def phase_d(self, sq):
        s = self.s
        dr = self.dr
        nc = self.nc
        act = self.view(A_ACT, [128, 11, S], BF16)
        b_act = Buf("act")
        ring = [self.view(A_D + 2048 * i, [128, 8, 128], BF16) for i in range(4)]
        b_ring = [Buf("ffr%d" % i) for i in range(4)]
        Wd = self.view(A_D + 8192, [128, 11, D], BF16)
        b_wd = Buf("wd")
        sgt = [self.view(A_D + 30720 + 2048 * i, [128, 512], F32) for i in range(2)]
        b_sgt = [Buf("sgt%d" % i) for i in range(2)]
        lng = self.view(A_D + 34816, [128, D], F32)
        lnb = self.view(A_D + 38912, [128, D], F32)
        b_ln = Buf("ln2")
        s.dma("sp", lng, dr["ln2_g"], writes=[b_ln])
        s.dma("sp", lnb, dr["ln2_b"], writes=[b_ln])
        st = [self.lnst[:, 32 * i:32 * i + 32] for i in range(4)]
        b_st = [Buf("lnst%d" % i) for i in range(4)]
        H, hT = self.H, self.hT

        def d_back(tt_):
            self.ln_back(H[:, tt_, :], self.b_H[tt_], lnb, b_ln, H[:, tt_, :], self.b_H[tt_])
            s.dma("sp", dr["out"][sq, tt_ * 128:(tt_ + 1) * 128, :], H[:, tt_, :], reads=[self.b_H[tt_]], writes=[])

        ri = 0
        for half in range(2):
            for hcl in range(11):
                hc = half * 11 + hcl
                w = []
                for nm in ("wg", "wu"):
                    j = ri % 4
                    ri += 1
                    s.dma("pool", ring[j], dr[nm][hc], writes=[b_ring[j]])
                    w.append((ring[j], b_ring[j]))
                if hcl == 1:
                    s.dma("pool", Wd, dr["wd"][:, half * 11:(half + 1) * 11, :], writes=[b_wd])
                for tc in range(4):
                    ts_ = slice(tc * 512, (tc + 1) * 512)
                    b0 = (tc % 2) * 2
                    for i in range(2):
                        for k in range(8):
                            self.mm(self.banks[b0 + i][:], w[i][0][:, k, :], hT[:, k, ts_], k == 0, k == 7,
                                    [w[i][1], self.b_hT], [self.b_bank[b0 + i]])
                    sgi = tc % 2
                    self.act(sgt[sgi], self.banks[b0][:], AF.Silu, [self.b_bank[b0]], [b_sgt[sgi]])
                    self.tt(act[:, hcl, ts_], sgt[sgi], self.banks[b0 + 1][:], ALU.mult,
                            [b_sgt[sgi], self.b_bank[b0 + 1]], [b_act])
            if half == 1 and sq + 1 < self.nseq and self.stop_after is None:
                self.load_xT(sq + 1, war=[self.b_hT])
            for tt_ in range(NT):
                tsl = slice(tt_ * 128, (tt_ + 1) * 128)
                if half == 0:
                    bset = 4 + (tt_ % 2) * 2
                else:
                    bset = (tt_ % 4) * 2
                for ch in range(2):
                    bi = bset + ch
                    for hcl in range(11):
                        self.mm(self.banks[bi][:], act[:, hcl, tsl], Wd[:, hcl, ch * 512:(ch + 1) * 512],
                                hcl == 0, hcl == 10, [b_act, b_wd], [self.b_bank[bi]])
                    hs = H[:, tt_, ch * 512:(ch + 1) * 512]
                    if half == 0:
                        self.stt(hs, hs, ALPHA, self.banks[bi][:], ALU.mult, ALU.add,
                                 [self.b_H[tt_], self.b_bank[bi]], [self.b_H[tt_]])
                    else:
                        self.tt(hs, hs, self.banks[bi][:], ALU.add, [self.b_H[tt_], self.b_bank[bi]], [self.b_H[tt_]])
                if half == 1:
                    self.ln_front(H[:, tt_, :], self.b_H[tt_], lng, b_ln, st[tt_ % 4], b_st[tt_ % 4])
                    if tt_ >= 1:
                        d_back(tt_ - 1)
            if half == 1:
                d_back(NT - 1)


def build_program(nseq=SEQ_PER_CORE, stop_after=None, dbg=()):
    kb = KB(nseq, stop_after, dbg)
    nc = kb.nc
    kb.lnst = None
    from contextlib import ExitStack
    with ExitStack() as es:
        kb.lnst = es.enter_context(nc.sbuf_tensor("sb_lnst", [128, 128], F32))
        kb.build()
    return kb


_CACHE = {}


def make_in_maps(inputs, ncores, nseq):
    consts = host_consts()
    wts = host_weights(inputs)
    x = np.asarray(inputs["x"], np.float32)
    maps = []
    for c in range(ncores):
        xs = np.ascontiguousarray(x[c * nseq:(c + 1) * nseq])
        m = {"x": xs, "xt": np.ascontiguousarray(xs.transpose(0, 2, 1))}
        m.update(consts)
        m.update(wts)
        maps.append(m)
    return maps


def kernel(**inputs):
    kb = build_program(SEQ_PER_CORE)
    maps = make_in_maps(inputs, NCORES, SEQ_PER_CORE)
    res = run_bass_kernel_spmd(kb.nc, maps, core_ids=list(range(NCORES)))
    out = np.concatenate([np.asarray(r["out"]) for r in res.results], axis=0)
    return out.astype(np.float32)
```
